# Optimizing a Trainium2 kernel written in Bass

```python
import jax, jax.numpy as jnp
from jax import lax
import numpy as np

D_MODEL = 1024
BATCH = 16
SEQ = 2048
DEPTH = 2

ATT_HEAD_DIM = 64
ATT_HEADS = 8
ATT_KV_HEADS = 2
ATT_GROUP = ATT_HEADS // ATT_KV_HEADS
ATT_WIDTH = ATT_HEADS * ATT_HEAD_DIM
ATT_KV_WIDTH = ATT_KV_HEADS * ATT_HEAD_DIM
WINDOW = 128
ATT_BLOCK = 128
ROPE_DIM = ATT_HEAD_DIM // 4
ROPE_THETA = 500000.0
MLSTM_HEADS = 4
MLSTM_HEAD_DIM = 128
MLSTM_WIDTH = MLSTM_HEADS * MLSTM_HEAD_DIM
MLSTM_CHUNK = 128
MLSTM_N_GATES = 4 * MLSTM_HEADS
CONV_K = 3
D_FF = ((8 * D_MODEL // 3 + 255) // 256) * 256
NORM_EPS = 1e-6
SPLIT_SIZES = (ATT_WIDTH, ATT_KV_WIDTH, ATT_KV_WIDTH, MLSTM_WIDTH, MLSTM_WIDTH, MLSTM_WIDTH, MLSTM_WIDTH, MLSTM_N_GATES, 2 * D_MODEL)
IN_WIDTH = sum(SPLIT_SIZES)

kernel_name = 'hybrid_bidir_swa_mlstm_macaron'


def rms_norm(x, g):
    xf = x.astype(jnp.float32)
    y = xf * lax.rsqrt(jnp.mean(xf * xf, axis=-1, keepdims=True) + NORM_EPS)
    return (y * g.astype(jnp.float32)).astype(x.dtype)


def swiglu(h, w_gate, w_up, w_down):
    return (jax.nn.silu(h @ w_gate) * (h @ w_up)) @ w_down


def partial_rope(t, positions):
    half = ROPE_DIM // 2
    inv_freq = jnp.power(jnp.float32(ROPE_THETA), -jnp.arange(half, dtype=jnp.float32) * (2.0 / ROPE_DIM))
    ang = positions.astype(jnp.float32)[:, :, None] * inv_freq
    cos = jnp.cos(ang)[:, :, None, :]
    sin = jnp.sin(ang)[:, :, None, :]
    tr = t[..., :ROPE_DIM].astype(jnp.float32)
    t1, t2 = tr[..., :half], tr[..., half:]
    rot = jnp.concatenate([t1 * cos - t2 * sin, t2 * cos + t1 * sin], axis=-1)
    return jnp.concatenate([rot.astype(t.dtype), t[..., ROPE_DIM:]], axis=-1)


def windowed_gqa_with_sink(q, k, v, sink):
    B, S, _, dh = q.shape
    wb = ATT_BLOCK
    nb = S // wb
    f32 = jnp.float32
    qb = q.astype(f32).reshape(B, nb, wb, ATT_KV_HEADS, ATT_GROUP, dh)
    pad = ((0, 0), (wb, wb), (0, 0), (0, 0))
    kp = jnp.pad(k.astype(f32), pad).reshape(B, nb + 2, wb, ATT_KV_HEADS, dh)
    vp = jnp.pad(v.astype(f32), pad).reshape(B, nb + 2, wb, ATT_KV_HEADS, dh)
    kb = jnp.concatenate([kp[:, :-2], kp[:, 1:-1], kp[:, 2:]], axis=2)
    vb = jnp.concatenate([vp[:, :-2], vp[:, 1:-1], vp[:, 2:]], axis=2)
    s = jnp.einsum('bnqhgd,bnkhd->bnhgqk', qb, kb) * (dh ** -0.5)
    qi = jnp.arange(nb)[:, None, None] * wb + jnp.arange(wb)[None, :, None]
    kj = jnp.arange(nb)[:, None, None] * wb - wb + jnp.arange(3 * wb)[None, None, :]
    valid = (jnp.abs(qi - kj) <= WINDOW) & (kj >= 0) & (kj < S)
    s = jnp.where(valid[None, :, None, None], s, -jnp.inf)
    sink_l = sink.astype(f32).reshape(1, 1, ATT_KV_HEADS, ATT_GROUP, 1, 1)
    m = jnp.maximum(jnp.max(s, axis=-1, keepdims=True), sink_l)
    p = jnp.exp(s - m)
    den = jnp.sum(p, axis=-1, keepdims=True) + jnp.exp(sink_l - m)
    o = jnp.einsum('bnhgqk,bnkhd->bnqhgd', p / den, vb)
    return o.reshape(B, S, ATT_HEADS * dh).astype(q.dtype)


def centred_depthwise_conv(u, w, b):
    S = u.shape[1]
    pad = CONV_K // 2
    up = jnp.pad(u, ((0, 0), (pad, pad), (0, 0)))
    out = up[:, 0:S] * w[0]
    for j in range(1, CONV_K):
        out = out + up[:, j:j + S] * w[j]
    return out + b


def mlstm_chunkwise(q, k, v, log_i, log_f):
    B, H, S, dk = q.shape
    dv = v.shape[-1]
    L = MLSTM_CHUNK
    nc = S // L
    q = q.reshape(B, H, nc, L, dk)
    k = k.reshape(B, H, nc, L, dk)
    v = v.reshape(B, H, nc, L, dv)
    li = log_i.reshape(B, H, nc, L)
    b = jnp.cumsum(log_f.reshape(B, H, nc, L), axis=-1)
    b_tot = b[..., -1]
    a = b_tot[..., None] - b + li
    a_max = jnp.max(a, axis=-1)
    w = jnp.exp(a - a_max[..., None])
    kw = k * w[..., None]
    C_loc = jnp.einsum('bhcsk,bhcsv->bhckv', kw, v)
    n_loc = jnp.sum(kw, axis=3)

    def step(carry, inp):
        C, n, m = carry
        C_l, n_l, am, bt = inp
        m_new = jnp.maximum(bt + m, am)
        s_p = jnp.exp(bt + m - m_new)
        s_l = jnp.exp(am - m_new)
        C_new = s_p[..., None, None] * C + s_l[..., None, None] * C_l
        n_new = s_p[..., None] * n + s_l[..., None] * n_l
        return (C_new, n_new, m_new), (C, n, m)

    init = (jnp.zeros((B, H, dk, dv), jnp.float32), jnp.zeros((B, H, dk), jnp.float32), jnp.zeros((B, H), jnp.float32))
    xs = (jnp.moveaxis(C_loc, 2, 0), jnp.moveaxis(n_loc, 2, 0), jnp.moveaxis(a_max, 2, 0), jnp.moveaxis(b_tot, 2, 0))
    _, (C_in, n_in, m_in) = lax.scan(step, init, xs)
    C_in = jnp.moveaxis(C_in, 0, 2)
    n_in = jnp.moveaxis(n_in, 0, 2)
    m_in = jnp.moveaxis(m_in, 0, 2)

    D = b[..., :, None] - b[..., None, :] + li[..., None, :]
    tri = jnp.tril(jnp.ones((L, L), dtype=bool))
    D = jnp.where(tri, D, -jnp.inf)
    inter = b + m_in[..., None]
    m_t = jnp.maximum(inter, jnp.max(D, axis=-1))
    P = jnp.exp(D - m_t[..., None])
    sc = jnp.einsum('bhctd,bhcsd->bhcts', q, k) * P
    scale_in = jnp.exp(inter - m_t)
    num = jnp.einsum('bhcts,bhcsv->bhctv', sc, v) + scale_in[..., None] * jnp.einsum('bhctk,bhckv->bhctv', q, C_in)
    den = jnp.sum(sc, axis=-1) + scale_in * jnp.einsum('bhctk,bhck->bhct', q, n_in)
    h = num / jnp.maximum(jnp.abs(den), jnp.exp(-m_t))[..., None]
    return h.reshape(B, H, S, dv)


def bidirectional_mlstm(q, k, v, o_pre, gate_pre, gate_bias, norm_g):
    B, S, _ = q.shape
    f32 = jnp.float32

    def heads(t):
        return t.astype(f32).reshape(B, S, MLSTM_HEADS, MLSTM_HEAD_DIM).transpose(0, 2, 1, 3)

    qh = heads(q)
    kh = heads(k) * (MLSTM_HEAD_DIM ** -0.5)
    vh = heads(v)
    g = (gate_pre.astype(f32) + gate_bias.astype(f32)).reshape(B, S, 4, MLSTM_HEADS).transpose(2, 0, 3, 1)
    h_fwd = mlstm_chunkwise(qh, kh, vh, g[0], jax.nn.log_sigmoid(g[1]))

    def flip(t):
        return jnp.flip(t, axis=2)

    h_bwd = flip(mlstm_chunkwise(flip(qh), flip(kh), flip(vh), flip(g[2]), jax.nn.log_sigmoid(flip(g[3]))))
    h = h_fwd + h_bwd
    mu = jnp.mean(h, axis=-1, keepdims=True)
    var = jnp.mean(jnp.square(h - mu), axis=-1, keepdims=True)
    h = (h - mu) * lax.rsqrt(var + NORM_EPS)
    h = h.transpose(0, 2, 1, 3).reshape(B, S, MLSTM_WIDTH) * norm_g.astype(f32)
    return (jax.nn.sigmoid(o_pre.astype(f32)) * h).astype(q.dtype)


def setup_inputs(seed: int = 0) -> dict:
    key = jax.random.key(seed)
    ks = jax.random.split(key, 26)
    f32 = jnp.float32
    L = DEPTH

    def dense(k, shape, fan_in):
        return jax.random.normal(k, shape, f32) * (fan_in ** -0.5)

    def gain(k, shape):
        return 1.0 + 0.02 * jax.random.normal(k, shape, f32)

    x = jax.random.normal(ks[0], (BATCH, SEQ, D_MODEL), f32)
    positions = jnp.arange(SEQ, dtype=jnp.int32)[None, :] + jax.random.randint(ks[1], (BATCH, 1), 0, 1024, dtype=jnp.int32)
    forget_base = jnp.linspace(3.0, 6.0, MLSTM_HEADS, dtype=f32)
    is_forget = jnp.array([0.0, 1.0, 0.0, 1.0], f32)
    mlstm_gate_bias = (0.1 * jax.random.normal(ks[8], (L, 4, MLSTM_HEADS), f32) + is_forget[None, :, None] * forget_base[None, None, :]).reshape(L, MLSTM_N_GATES)
    return {
        'x': x,
        'positions': positions,
        'ffn1_norm': gain(ks[2], (L, D_MODEL)),
        'ffn1_w_gate': dense(ks[3], (L, D_MODEL, D_FF), D_MODEL),
        'ffn1_w_up': dense(ks[4], (L, D_MODEL, D_FF), D_MODEL),
        'ffn1_w_down': dense(ks[5], (L, D_FF, D_MODEL), D_FF),
        'mix_norm': gain(ks[6], (L, D_MODEL)),
        'w_in': dense(ks[7], (L, D_MODEL, IN_WIDTH), D_MODEL),
        'mlstm_gate_bias': mlstm_gate_bias,
        'attn_q_norm': gain(ks[9], (L, ATT_HEAD_DIM)),
        'attn_k_norm': gain(ks[10], (L, ATT_HEAD_DIM)),
        'attn_sink': 0.5 * jax.random.normal(ks[11], (L, ATT_HEADS), f32),
        'mlstm_conv_w': dense(ks[12], (L, CONV_K, 2 * MLSTM_WIDTH), CONV_K),
        'mlstm_conv_b': 0.02 * jax.random.normal(ks[13], (L, 2 * MLSTM_WIDTH), f32),
        'mlstm_out_norm': gain(ks[14], (L, MLSTM_WIDTH)),
        'w_branch_attn': dense(ks[15], (L, ATT_WIDTH, D_MODEL), ATT_WIDTH),
        'w_branch_mlstm': dense(ks[16], (L, MLSTM_WIDTH, D_MODEL), MLSTM_WIDTH),
        'w_out': dense(ks[17], (L, D_MODEL, D_MODEL), D_MODEL),
        'ffn2_norm': gain(ks[18], (L, D_MODEL)),
        'ffn2_w_gate': dense(ks[19], (L, D_MODEL, D_FF), D_MODEL),
        'ffn2_w_up': dense(ks[20], (L, D_MODEL, D_FF), D_MODEL),
        'ffn2_w_down': dense(ks[21], (L, D_FF, D_MODEL), D_FF),
        'block_out_norm': gain(ks[22], (L, D_MODEL)),
    }


def reference(x, positions, ffn1_norm, ffn1_w_gate, ffn1_w_up, ffn1_w_down, mix_norm, w_in, mlstm_gate_bias, attn_q_norm, attn_k_norm, attn_sink, mlstm_conv_w, mlstm_conv_b, mlstm_out_norm, w_branch_attn, w_branch_mlstm, w_out, ffn2_norm, ffn2_w_gate, ffn2_w_up, ffn2_w_down, block_out_norm):
    B, S, _ = x.shape
    split_idx = np.cumsum(SPLIT_SIZES)[:-1].tolist()
    for l in range(DEPTH):
        x = x + 0.5 * swiglu(rms_norm(x, ffn1_norm[l]), ffn1_w_gate[l], ffn1_w_up[l], ffn1_w_down[l])

        h = rms_norm(x, mix_norm[l])
        proj = h @ w_in[l]
        qa, ka, va, qm, km, vm, om, gm, gmerge = jnp.split(proj, split_idx, axis=-1)

        qa = rms_norm(qa.reshape(B, S, ATT_HEADS, ATT_HEAD_DIM), attn_q_norm[l])
        ka = rms_norm(ka.reshape(B, S, ATT_KV_HEADS, ATT_HEAD_DIM), attn_k_norm[l])
        qa = partial_rope(qa, positions)
        ka = partial_rope(ka, positions)
        va = va.reshape(B, S, ATT_KV_HEADS, ATT_HEAD_DIM)
        y_a = windowed_gqa_with_sink(qa, ka, va, attn_sink[l])

        qk = jax.nn.silu(centred_depthwise_conv(jnp.concatenate([qm, km], axis=-1), mlstm_conv_w[l], mlstm_conv_b[l]))
        qm, km = jnp.split(qk, 2, axis=-1)
        y_m = bidirectional_mlstm(qm, km, vm, om, gm, mlstm_gate_bias[l], mlstm_out_norm[l])

        g_a, g_m = jnp.split(jax.nn.sigmoid(gmerge), 2, axis=-1)
        merged = g_a * (y_a @ w_branch_attn[l]) + g_m * (y_m @ w_branch_mlstm[l])
        x = x + merged @ w_out[l]

        x = x + 0.5 * swiglu(rms_norm(x, ffn2_norm[l]), ffn2_w_gate[l], ffn2_w_up[l], ffn2_w_down[l])
        x = rms_norm(x, block_out_norm[l])
    return x
```

```python
import contextlib
import numpy as np
import ml_dtypes
import concourse.bass as bass
import concourse.mybir as mybir
from concourse.bass_utils import run_bass_kernel_spmd

F32 = mybir.dt.float32
BF16 = mybir.dt.bfloat16
I32 = mybir.dt.int32
AF = mybir.ActivationFunctionType
ALU = mybir.AluOpType
AX = mybir.AxisListType

ENGS = ("pe", "act", "dve", "pool", "sp")

D = 1024
S = 2048
L = 2
DFF = 2816
NJ = DFF // 128
NCORES = 8
SEQ_PER_CORE = 2
EPS = 1e-6
NEG = -30000.0


class Buf:
    __slots__ = ("name", "writers", "readers")

    def __init__(self, name=""):
        self.name = name
        self.writers = []
        self.readers = []


class Ctx:
    def __init__(self, nc, es):
        self.nc = nc
        self.es = es
        self.streams = {e: [] for e in ENGS}
        self.cnt = {e: 0 for e in ENGS}
        self.sem = {}
        for e in ENGS:
            self.sem[e] = es.enter_context(nc.semaphore("sem_" + e))
        self.waited = {e: {} for e in ENGS}
        self.dma_sems = {}
        self.dma_cnt = {}

    def sb(self, name, shape, dt):
        return self.es.enter_context(self.nc.sbuf_tensor(name, list(shape), dt))

    def ps(self, name, shape, dt):
        return self.es.enter_context(self.nc.psum_tensor(name, list(shape), dt))

    def dma_sem(self, key):
        if key not in self.dma_sems:
            self.dma_sems[key] = self.es.enter_context(self.nc.semaphore("dsem_" + str(key)))
            self.dma_cnt[key] = 0
        return self.dma_sems[key]

    def _deps(self, reads, writes):
        deps = []
        for b in reads:
            deps.extend(b.writers)
        for b in writes:
            deps.extend(b.writers)
            deps.extend(b.readers)
        return deps

    def _waits(self, eng, deps):
        best = {}
        for (k, v) in deps:
            if k == ("eng", "pe") and eng == "pe":
                continue
            if v > best.get(k, 0):
                best[k] = v
        out = []
        for k, v in best.items():
            if self.waited[eng].get(k, 0) >= v:
                continue
            self.waited[eng][k] = v
            sem = self.sem[k[1]] if k[0] == "eng" else self.dma_sems[k[1]]
            out.append((sem, v))
        return out

    def _record(self, ev, reads, writes):
        for b in reads:
            if len(b.readers) > 24:
                best = {}
                for (k, v) in b.readers:
                    if v > best.get(k, 0):
                        best[k] = v
                b.readers = list(best.items())
            b.readers.append(ev)
        for b in writes:
            b.writers = [ev]
            b.readers = []

    def op(self, eng, fn, reads=(), writes=(), inc=True):
        deps = self._deps(reads, writes)
        waits = self._waits(eng, deps)
        if inc:
            self.cnt[eng] += 1
            ev = (("eng", eng), self.cnt[eng])
        else:
            ev = (("eng", eng), self.cnt[eng] + 1)
        sem = self.sem[eng]

        def run(e, fn=fn, waits=waits, inc=inc, sem=sem):
            for (s, v) in waits:
                e.wait_ge(s, v)
            ins = fn(e)
            if inc:
                ins.then_inc(sem, 1)

        self.streams[eng].append(run)
        self._record(ev, reads, writes)
        return ev

    def dma(self, eng, key, out, in_, reads=(), writes=()):
        sem = self.dma_sem(key)
        deps = self._deps(reads, writes)
        waits = self._waits(eng, deps)
        self.dma_cnt[key] += 16
        ev = (("dma", key), self.dma_cnt[key])

        def run(e, waits=waits, sem=sem, out=out, in_=in_):
            for (s, v) in waits:
                e.wait_ge(s, v)
            e.dma_start(out=out, in_=in_).then_inc(sem, 16)

        self.streams[eng].append(run)
        self._record(ev, reads, writes)
        return ev

    def barrier(self):
        targets = [(("eng", e), self.cnt[e]) for e in ENGS if self.cnt[e] > 0]
        targets += [(("dma", k), v) for k, v in self.dma_cnt.items() if v > 0]
        for eng in ENGS:
            waits = self._waits(eng, targets)

            def run(e, waits=waits):
                for (s_, v) in waits:
                    e.wait_ge(s_, v)

            self.streams[eng].append(run)

    def final_wait(self, eng, bufs):
        deps = []
        for b in bufs:
            deps.extend(b.writers)
        waits = self._waits(eng, deps)

        def run(e, waits=waits):
            for (s, v) in waits:
                e.wait_ge(s, v)

        self.streams[eng].append(run)

    def emit(self):
        nc = self.nc
        st = self.streams
        with nc.Block() as block:
            @block.tensor
            def _(e):
                for f in st["pe"]:
                    f(e)

            @block.scalar
            def _(e):
                for f in st["act"]:
                    f(e)

            @block.vector
            def _(e):
                for f in st["dve"]:
                    f(e)

            @block.gpsimd
            def _(e):
                for f in st["pool"]:
                    f(e)

            @block.sync
            def _(e):
                for f in st["sp"]:
                    f(e)

    def mm(self, out, lhsT, rhs, start, stop, reads=(), writes=(), inc=None):
        if inc is None:
            inc = stop
        return self.op("pe", lambda e: e.matmul(out, lhsT, rhs, start=start, stop=stop),
                       reads=reads, writes=writes, inc=inc)

    def tr(self, out, in_, ident, reads=(), writes=(), inc=True):
        return self.op("pe", lambda e: e.transpose(out, in_, ident), reads=reads, writes=writes, inc=inc)


    def act(self, out, in_, func, bias=None, scale=None, accum_out=None, reads=(), writes=()):
        kw = {}
        if bias is not None:
            kw["bias"] = bias
        if scale is not None:
            kw["scale"] = scale
        if accum_out is not None:
            kw["accum_out"] = accum_out
        return self.op("act", lambda e: e.activation(out=out, in_=in_, func=func, **kw), reads=reads, writes=writes)

    def tt(self, eng, out, in0, in1, op, reads=(), writes=()):
        return self.op(eng, lambda e: e.tensor_tensor(out=out, in0=in0, in1=in1, op=op), reads=reads, writes=writes)

    def stt(self, out, in0, scalar, in1, op0, op1, reads=(), writes=()):
        return self.op("dve", lambda e: e.scalar_tensor_tensor(out=out, in0=in0, scalar=scalar, in1=in1,
                                                               op0=op0, op1=op1), reads=reads, writes=writes)

    def ts(self, eng, out, in0, s1, s2, op0, op1=None, reads=(), writes=()):
        if op1 is None:
            return self.op(eng, lambda e: e.tensor_scalar(out=out, in0=in0, scalar1=s1, scalar2=None, op0=op0),
                           reads=reads, writes=writes)
        return self.op(eng, lambda e: e.tensor_scalar(out=out, in0=in0, scalar1=s1, scalar2=s2, op0=op0, op1=op1),
                       reads=reads, writes=writes)

    def copy(self, eng, out, in_, reads=(), writes=()):
        if eng == "act":
            return self.op("act", lambda e: e.activation(out=out, in_=in_, func=AF.Copy), reads=reads, writes=writes)
        return self.op(eng, lambda e: e.tensor_copy(out, in_), reads=reads, writes=writes)

    def red(self, out, in_, op, axis=None, reads=(), writes=()):
        ax = AX.X if axis is None else axis
        return self.op("dve", lambda e: e.tensor_reduce(out=out, in_=in_, axis=ax, op=op), reads=reads, writes=writes)

    def recip(self, out, in_, reads=(), writes=()):
        return self.op("dve", lambda e: e.reciprocal(out, in_), reads=reads, writes=writes)


class Rot:
    def __init__(self, items):
        self.items = items
        self.i = 0

    def next(self):
        it = self.items[self.i % len(self.items)]
        self.i += 1
        return it


CB = {}
_off = 0
for _n, _w in (("ident", 128), ("ones", 128), ("blk64", 128), ("rblk", 128),
               ("mprev", 512), ("mnext", 512), ("maskF", 128), ("maskB", 128)):
    CB[_n] = (_off, _w)
    _off += _w
NCB = _off

CF = {}
_off = 0
for _n, _w in (("ident", 128), ("LT", 128), ("UT", 128), ("ones", 128), ("invf", 1), ("eps", 1),
               ("one", 1), ("pi", 1), ("negpi", 1), ("twopi", 1), ("zero", 1), ("half", 1)):
    CF[_n] = (_off, _w)
    _off += _w
NCF = _off

PAR = {}
_off = 0
for _n, _w in (("g_ffn1", L * 8), ("g_mix", L * 8), ("g_ffn2", L * 8), ("g_out", L * 8),
               ("gq", L), ("gk", L), ("convw", L * 3 * 8), ("convb", L * 8), ("gmn", L * 4),
               ("gbias", L * 16), ("sink", L * 8)):
    PAR[_n] = (_off, _w)
    _off += _w
NPAR = _off


def make_consts():
    cb = np.zeros((128, NCB), np.float32)
    cf = np.zeros((128, NCF), np.float32)
    I = np.eye(128, dtype=np.float32)
    cb[:, CB["ident"][0]:CB["ident"][0] + 128] = I
    cb[:, CB["ones"][0]:CB["ones"][0] + 128] = 1.0
    blk = np.zeros((128, 128), np.float32)
    blk[:64, :64] = 1.0
    blk[64:, 64:] = 1.0
    cb[:, CB["blk64"][0]:CB["blk64"][0] + 128] = blk
    R = np.zeros((128, 128), np.float32)
    for m in range(128):
        d = m % 64
        if d < 8:
            R[m + 8, m] = -1.0
        elif d < 16:
            R[m - 8, m] = 1.0
    cb[:, CB["rblk"][0]:CB["rblk"][0] + 128] = R
    k = np.arange(128)[:, None]
    q = np.arange(128)[None, :]
    mprev = np.where(k >= q, 0.0, NEG).astype(np.float32)
    mnext = np.where(k <= q, 0.0, NEG).astype(np.float32)
    cb[:, CB["mprev"][0]:CB["mprev"][0] + 512] = np.tile(mprev, (1, 4))
    cb[:, CB["mnext"][0]:CB["mnext"][0] + 512] = np.tile(mnext, (1, 4))
    maskF = (k <= q).astype(np.float32)
    maskB = (k >= q).astype(np.float32)
    cb[:, CB["maskF"][0]:CB["maskF"][0] + 128] = maskF
    cb[:, CB["maskB"][0]:CB["maskB"][0] + 128] = maskB
    cf[:, CF["ident"][0]:CF["ident"][0] + 128] = I
    cf[:, CF["LT"][0]:CF["LT"][0] + 128] = maskF
    cf[:, CF["UT"][0]:CF["UT"][0] + 128] = maskB
    cf[:, CF["ones"][0]:CF["ones"][0] + 128] = 1.0
    half = 8
    inv_freq = np.power(np.float32(500000.0), -np.arange(half, dtype=np.float32) * np.float32(2.0 / 16)).astype(np.float32)
    invf = np.zeros(128, np.float32)
    for p in range(128):
        d = p % 64
        if d < 16:
            invf[p] = inv_freq[d % 8]
    cf[:, CF["invf"][0]] = invf
    cf[:, CF["eps"][0]] = EPS
    cf[:, CF["one"][0]] = 1.0
    cf[:, CF["pi"][0]] = np.pi
    cf[:, CF["negpi"][0]] = -np.pi
    cf[:, CF["twopi"][0]] = 2 * np.pi
    cf[:, CF["zero"][0]] = 0.0
    cf[:, CF["half"][0]] = 0.5
    return cb.astype(ml_dtypes.bfloat16), cf


def tile_w(W, KC):
    K, N = W.shape
    assert K == KC * 128 and N % 128 == 0
    return np.ascontiguousarray(
        W.reshape(KC, 128, N // 128, 128).transpose(2, 1, 0, 3).reshape(N // 128, 128, KC * 128))


def fm(v):
    return np.ascontiguousarray(v.reshape(-1, 128).T)


def prep_weights(inp):
    f32 = np.float32
    out = {}
    for f in (1, 2):
        wg = inp[f"ffn{f}_w_gate"].astype(f32, copy=False)
        wu = inp[f"ffn{f}_w_up"].astype(f32, copy=False)
        wd = inp[f"ffn{f}_w_down"].astype(f32, copy=False)
        gu = np.empty((L, NJ, 128, 2, 1024), f32)
        dd = np.empty((L, 8, 128, DFF), f32)
        for l in range(L):
            gu[l, :, :, 0, :] = tile_w(wg[l], 8)
            gu[l, :, :, 1, :] = tile_w(wu[l], 8)
            dd[l] = tile_w(wd[l], NJ)
        out[f"wgu{f}"] = gu.reshape(L, NJ, 128, 2048)
        out[f"wd{f}"] = dd
    w_in = inp["w_in"].astype(f32, copy=False)
    cols = []
    for i in range(4):
        cols.append(np.concatenate([np.arange(i * 64, i * 64 + 64), np.arange((i + 4) * 64, (i + 4) * 64 + 64)]))
    cols.append(np.arange(512, 640))
    cols.append(np.arange(640, 768))
    base = 768
    for h in range(4):
        for t in range(4):
            cols.append(np.arange(base + t * 512 + h * 128, base + t * 512 + h * 128 + 128))
    gbase = 768 + 2048 + 16
    for m in range(16):
        cols.append(np.arange(gbase + m * 128, gbase + m * 128 + 128))
    cols = np.concatenate(cols)
    nb = len(cols) // 128
    winb = np.empty((L, nb, 128, 1024), f32)
    wing = np.empty((L, 128, 8 * 16), f32)
    for l in range(L):
        winb[l] = tile_w(w_in[l][:, cols], 8)
        wg_ = w_in[l][:, 768 + 2048:768 + 2048 + 16]
        wing[l] = wg_.reshape(8, 128, 16).transpose(1, 0, 2).reshape(128, 128)
    out["winb"] = winb
    out["wing"] = wing
    out["wba"] = np.stack([tile_w(inp["w_branch_attn"][l].astype(f32, copy=False), 4) for l in range(L)])
    out["wbm"] = np.stack([tile_w(inp["w_branch_mlstm"][l].astype(f32, copy=False), 4) for l in range(L)])
    out["wo"] = np.stack([tile_w(inp["w_out"][l].astype(f32, copy=False), 8) for l in range(L)])
    par = np.zeros((128, NPAR), f32)

    def put(name, arr):
        o, w = PAR[name]
        arr = np.asarray(arr, f32).reshape(128, -1)
        assert arr.shape[1] == w, (name, arr.shape, w)
        par[:, o:o + w] = arr

    put("g_ffn1", np.concatenate([fm(inp["ffn1_norm"][l]) for l in range(L)], axis=1))
    put("g_mix", np.concatenate([fm(inp["mix_norm"][l]) for l in range(L)], axis=1))
    put("g_ffn2", np.concatenate([fm(inp["ffn2_norm"][l]) for l in range(L)], axis=1))
    put("g_out", np.concatenate([fm(inp["block_out_norm"][l]) for l in range(L)], axis=1))
    put("gq", np.stack([np.tile(inp["attn_q_norm"][l], 2) for l in range(L)], axis=1))
    put("gk", np.stack([np.tile(inp["attn_k_norm"][l], 2) for l in range(L)], axis=1))
    cw = np.zeros((128, L, 3, 8), f32)
    cbias = np.zeros((128, L, 8), f32)
    for l in range(L):
        for j in range(3):
            cw[:, l, j, :] = fm(inp["mlstm_conv_w"][l, j])
        cbias[:, l, :] = fm(inp["mlstm_conv_b"][l])
    put("convw", cw)
    put("convb", cbias)
    put("gmn", np.concatenate([fm(inp["mlstm_out_norm"][l]) for l in range(L)], axis=1))
    put("gbias", np.broadcast_to(inp["mlstm_gate_bias"].reshape(1, L * 16), (128, L * 16)))
    put("sink", np.broadcast_to(inp["attn_sink"].reshape(1, L * 8), (128, L * 8)))
    out["par"] = par
    return out


def build(layers=(0, 1), nseq=SEQ_PER_CORE, do_ffn1=True, do_mixer=True, do_ffn2=True, do_norm=True,
          TF=1024, mix_attn=True, mix_mlstm=True):
    nc = bass.Bass("TRN2", target_bir_lowering=False)
    dram = {}

    def din(name, shape, dt=F32):
        dram[name] = nc.dram_tensor(name, list(shape), dt, kind="ExternalInput").ap()
        return dram[name]

    x_d = din("x", [nseq, S, D])
    pos_d = din("pos", [nseq, 128, S], I32)
    cb_d = din("cb", [128, NCB], BF16)
    cf_d = din("cf", [128, NCF])
    par_d = din("par", [128, NPAR])
    wgu_d = {1: din("wgu1", [L, NJ, 128, 2048]), 2: din("wgu2", [L, NJ, 128, 2048])}
    wd_d = {1: din("wd1", [L, 8, 128, DFF]), 2: din("wd2", [L, 8, 128, DFF])}
    winb_d = din("winb", [L, 38, 128, 1024])
    wing_d = din("wing", [L, 128, 128])
    wba_d = din("wba", [L, 8, 128, 512])
    wbm_d = din("wbm", [L, 8, 128, 512])
    wo_d = din("wo", [L, 8, 128, 1024])
    out_d = nc.dram_tensor("out", [nseq, S, D], F32, kind="ExternalOutput").ap()

    with contextlib.ExitStack() as es:
        c = Ctx(nc, es)
        xT = c.sb("xT", [128, 8, S], F32)
        xB = [[Buf(f"xT{m}_{t}") for t in range(S // 512)] for m in range(8)]
        cb = c.sb("cb_s", [128, NCB], BF16)
        cf = c.sb("cf_s", [128, NCF], F32)
        par = c.sb("par_s", [128, NPAR], F32)
        cstB = Buf("consts")

        def CBs(name):
            o, w = CB[name]
            return cb[:, o:o + w]

        def CFs(name):
            o, w = CF[name]
            return cf[:, o:o + w]

        def PARs(name, i, n=1):
            o, w = PAR[name]
            return par[:, o + i:o + i + n]

        psum = [c.ps(f"ps{i}", [128, 512], F32) for i in range(8)]
        psB = [Buf(f"ps{i}") for i in range(8)]

        c.dma("sp", "cst", cb[:], cb_d[:, :], writes=[cstB])
        c.dma("sp", "cst", cf[:], cf_d[:, :], writes=[cstB])
        c.dma("sp", "cst", par[:], par_d[:, :], writes=[cstB])

        AW = 35200
        arena = c.sb("arena", [128, AW], F32)
        apos = [0]

        def carve(shape, dt):
            n = int(np.prod(shape))
            words = (n + 1) // 2 if dt == BF16 else n
            o = apos[0]
            assert o + words <= AW, ("arena overflow", o, words, AW)
            apos[0] = o + words
            v = arena[:, o:o + words]
            if dt == BF16:
                v = v.bitcast(BF16)[:, 0:n]
            elif dt == I32:
                v = v.bitcast(I32)
            if len(shape) == 2:
                v = v.rearrange("p (a b) -> p a b", a=shape[0])
            elif len(shape) == 3:
                v = v.rearrange("p (a b c) -> p a b c", a=shape[0], b=shape[1])
            return v

        def phase_reset(to=0):
            c.barrier()
            apos[0] = to

        class FFNBufs:
            pass

        NB = {}

        def alloc_norm(nsets=2):
            NB["sets"] = []
            for i in range(nsets):
                NB["sets"].append(dict(sq=carve([8, 512], BF16), sqB=Buf(f"sq{i}"), lnv=carve([512], F32), lnvB=Buf(f"lnv{i}"),
                                       rstd=carve([512], F32), rstdB=Buf(f"rstd{i}")))
            NB["i"] = 0

        def alloc_ffn():
            fb = FFNBufs()
            alloc_norm(1)
            fb.hT = [carve([8, TF], BF16) for i in range(2)]
            fb.hTB = [Buf(f"hT{i}") for i in range(2)]
            fb.actT = carve([NJ, TF], BF16)
            fb.actB = [Buf(f"act{j}") for j in range(NJ)]
            sg = [carve([512], F32) for i in range(2)]
            fb.sgR = Rot([(sg[i], Buf(f"sg{i}")) for i in range(2)])
            NGU = 3
            fb.wguR = Rot([(carve([2048], BF16), Buf(f"wgu_s{i}"), f"wgu{i}") for i in range(NGU)])
            NWD = 2
            fb.wdR = Rot([(carve([DFF], BF16), Buf(f"wd_s{i}"), f"wd{i}") for i in range(NWD)])
            return fb

        def alloc_io():
            return Rot([(carve([D], F32), Buf(f"xst{i}"), f"xst{i}") for i in range(2)])

        psGU = Rot([(psum[i], psB[i]) for i in (0, 1, 2, 3)])
        psDN = Rot([(psum[i], psB[i]) for i in (4, 5)])
        psN = Rot([(psum[i], psB[i]) for i in (6, 7)])
        psIO = Rot([(psum[i], psB[i]) for i in (0, 1, 2, 3)])
        evac_rr = [0]

        def evac_eng():
            evac_rr[0] += 1
            return "act" if evac_rr[0] % 2 else "dve"

        def copy_op(eng, out, in_, reads, writes):
            return c.copy(eng, out, in_, reads=reads, writes=writes)

        def load_x(seq):
            phase_reset()
            xstR = alloc_io()
            for tt in range(S // 128):
                st, stB, key = xstR.next()
                c.dma("sp", key, st[:], x_d[seq, tt * 128:(tt + 1) * 128, :], writes=[stB])
                for half in range(2):
                    ps, pB = psIO.next()
                    for k in range(4):
                        kc = half * 4 + k
                        c.tr(ps[:, k * 128:(k + 1) * 128], st[:, kc * 128:(kc + 1) * 128], CFs("ident"),
                             reads=[stB, cstB], writes=[pB], inc=(k == 3))
                    dst = xT[:, half * 4:half * 4 + 4, tt * 128:(tt + 1) * 128]
                    src = ps[:, :].rearrange("p (k t) -> p k t", k=4)
                    copy_op(evac_eng(), dst, src, [pB], [xB[half * 4 + k][tt // 4] for k in range(4)])

        def store_x(seq):
            phase_reset()
            xstR = alloc_io()
            outB = Buf("outdram")
            for tt in range(S // 128):
                st, stB, key = xstR.next()
                for half in range(2):
                    ps, pB = psIO.next()
                    for k in range(4):
                        kc = half * 4 + k
                        c.tr(ps[:, k * 128:(k + 1) * 128], xT[:, kc, tt * 128:(tt + 1) * 128], CFs("ident"),
                             reads=[xB[kc][tt // 4], cstB], writes=[pB], inc=(k == 3))
                    copy_op(evac_eng(), st[:, half * 512:(half + 1) * 512], ps[:, :], [pB], [stB])
                c.dma("sp", key + "o", out_d[seq, tt * 128:(tt + 1) * 128, :], st[:], reads=[stB], writes=[outB])
            return outB

        def rms_sq(tt):
            st = NB["sets"][NB["i"] % len(NB["sets"])]
            NB["i"] += 1
            NB["cur"] = st
            tok = slice(tt * 512, (tt + 1) * 512)
            for kc in range(8):
                c.act(st["sq"][:, kc, :], xT[:, kc, tok], AF.Square, reads=[xB[kc][tt]], writes=[st["sqB"]])

        def rms_mm(tt):
            st = NB["cur"]
            sq, sqB, lnv, lnvB, rstd, rstdB = st["sq"], st["sqB"], st["lnv"], st["lnvB"], st["rstd"], st["rstdB"]
            ps, pB = psN.next()
            for kc in range(8):
                c.mm(ps[:, :], CBs("ones"), sq[:, kc, :], start=(kc == 0), stop=(kc == 7),
                     reads=[sqB, cstB], writes=[pB])
            c.act(lnv, ps[:, :], AF.Ln, bias=CFs("eps"), scale=1.0 / D, reads=[pB, cstB], writes=[lnvB])
            c.act(rstd, lnv, AF.Exp, scale=-0.5, reads=[lnvB], writes=[rstdB])

        def rms_stats(tt):
            rms_sq(tt)
            rms_mm(tt)

        def rms_apply(tt, gname, l, dst_fn, dstB_fn):
            st = NB["cur"]
            rstd, rstdB = st["rstd"], st["rstdB"]
            tok = slice(tt * 512, (tt + 1) * 512)
            for kc in range(8):
                g = PARs(gname, l * 8 + kc)
                c.stt(dst_fn(kc), xT[:, kc, tok], g, rstd, ALU.mult, ALU.mult,
                      reads=[xB[kc][tt], rstdB, cstB], writes=dstB_fn(kc))

        def ffn(l, f):
            phase_reset()
            fb = alloc_ffn()
            hT, hTB, actT, actB, sgR, wguR, wdR = fb.hT, fb.hTB, fb.actT, fb.actB, fb.sgR, fb.wguR, fb.wdR
            gname = "g_ffn%d" % f
            ntt = S // TF
            nsub = TF // 512
            hsel = [0]

            def norm(tt):
                i = hsel[0] % 2
                hsel[0] += 1
                for sub in range(nsub):
                    t5 = tt * nsub + sub
                    rms_stats(t5)
                    rms_apply(t5, gname, l, lambda kc: hT[i][:, kc, sub * 512:(sub + 1) * 512], lambda kc: [hTB[i]])
                return i

            cur = norm(0)
            for tt in range(ntt):
                h, hB = hT[cur], hTB[cur]
                for j in range(NJ):
                    ws, wB, key = wguR.next()
                    c.dma("pool", key, ws, wgu_d[f][l, j], writes=[wB])
                    for sub in range(nsub):
                        ss_ = slice(sub * 512, (sub + 1) * 512)
                        pg, pgB = psGU.next()
                        pu, puB = psGU.next()
                        for g, (ps, pB) in enumerate(((pg, pgB), (pu, puB))):
                            for kc in range(8):
                                o = (g * 8 + kc) * 128
                                c.mm(ps[:, :], ws[:, o:o + 128], h[:, kc, ss_], start=(kc == 0), stop=(kc == 7),
                                     reads=[wB, hB], writes=[pB])
                        s_, sB_ = sgR.next()
                        c.act(s_, pg[:, :], AF.Silu, reads=[pgB], writes=[sB_])
                        c.tt("dve", actT[:, j, ss_], pu[:, :], s_, ALU.mult, reads=[puB, sB_], writes=[actB[j]])
                    if tt + 1 < ntt and nsub == 2:
                        t5a, t5b = (tt + 1) * nsub, (tt + 1) * nsub + 1
                        if j == 3:
                            nxt = hsel[0] % 2
                            hsel[0] += 1
                            rms_sq(t5a)
                        elif j == 6:
                            rms_mm(t5a)
                        elif j == 8:
                            rms_apply(t5a, gname, l, lambda kc: hT[nxt][:, kc, 0:512], lambda kc: [hTB[nxt]])
                        elif j == 10:
                            rms_sq(t5b)
                        elif j == 13:
                            rms_mm(t5b)
                        elif j == 15:
                            rms_apply(t5b, gname, l, lambda kc: hT[nxt][:, kc, 512:1024], lambda kc: [hTB[nxt]])
                if tt + 1 < ntt and nsub != 2:
                    nxt = norm(tt + 1)
                for m in range(8):
                    ws, wB, key = wdR.next()
                    c.dma("pool", key, ws, wd_d[f][l, m], writes=[wB])
                    for sub in range(nsub):
                        ss_ = slice(sub * 512, (sub + 1) * 512)
                        t5 = tt * nsub + sub
                        tok = slice(t5 * 512, (t5 + 1) * 512)
                        ps, pB = psDN.next()
                        for j in range(NJ):
                            c.mm(ps[:, :], ws[:, j * 128:(j + 1) * 128], actT[:, j, ss_], start=(j == 0), stop=(j == NJ - 1),
                                 reads=[wB, actB[j]], writes=[pB])
                        c.stt(xT[:, m, tok], ps[:, :], 0.5, xT[:, m, tok], ALU.mult, ALU.add,
                              reads=[pB], writes=[xB[m][t5]])
                if tt + 1 < ntt:
                    cur = nxt

        def final_norm(l):
            phase_reset()
            alloc_norm()
            nt = S // 512
            rms_stats(0)
            for tt in range(nt):
                tok = slice(tt * 512, (tt + 1) * 512)
                cur = NB["cur"]
                if tt + 1 < nt:
                    rms_stats(tt + 1)
                nxt = NB["cur"]
                NB["cur"] = cur
                rms_apply(tt, "g_out", l, lambda kc: xT[:, kc, tok], lambda kc: [xB[kc][tt]])
                NB["cur"] = nxt

        psM = Rot([(psum[i], psB[i]) for i in range(8)])
        CMUL = float(128 ** -0.5)

        def wload(R_, src):
            ws, wB, key = R_.next()
            n = src.shape[-1]
            c.dma("pool", key, ws[:, 0:n], src, writes=[wB])
            return ws, wB

        def mixer(l, seq, mix_attn=True, mix_mlstm=True):
            phase_reset()
            hTm = carve([8, S], BF16)
            hTmB = [Buf(f"hTm{t}") for t in range(4)]
            y_mT = carve([4, S], BF16)
            ymB = [Buf(f"ym{t}") for t in range(4)]
            wmR = Rot([(carve([1024], BF16), Buf(f"wm{i}"), f"wm{i}") for i in range(4)])
            mark0 = apos[0]
            alloc_norm()
            rms_stats(0)
            for tt in range(4):
                tok = slice(tt * 512, (tt + 1) * 512)
                cur = NB["cur"]
                if tt + 1 < 4:
                    rms_stats(tt + 1)
                nxt = NB["cur"]
                NB["cur"] = cur
                rms_apply(tt, "g_mix", l, lambda kc: hTm[:, kc, tok], lambda kc: [hTmB[tt]])
                NB["cur"] = nxt

            phase_reset(mark0)
            if mix_mlstm:
                mlstm(l, hTm, hTmB, y_mT, ymB, wmR)
            phase_reset(mark0)
            y_aT = carve([4, S], BF16)
            yaB = [Buf(f"ya{t}") for t in range(4)]
            if mix_attn:
                attention(l, seq, hTm, hTmB, y_aT, yaB, wmR)
            phase_reset(apos[0] if not mix_attn else mark_after_ya[0])
            merge(l, hTm, hTmB, y_aT, yaB, y_mT, ymB, wmR, mix_attn, mix_mlstm)

        mark_after_ya = [0]

        def mlstm(l, hTm, hTmB, y_mT, ymB, wmR):
            gB = Buf("gates")
            dcy = carve([2, 16, 4], F32)
            Eg = carve([2, 16, 4], F32)
            flo = carve([2, 16, 4], F32)
            HS = []
            for i in range(2):
                st = {}
                st["qT"] = carve([S], BF16)
                st["kT"] = carve([S], BF16)
                st["qB"] = Buf(f"qT{i}")
                st["kB"] = Buf(f"kT{i}")
                st["k_tm"] = carve([16, 128], BF16)
                st["ktB"] = Buf(f"k_tm{i}")
                st["vaug"] = carve([16, 130], BF16)
                st["vB"] = Buf(f"vaug{i}")
                HS.append(st)
            ub = carve([S], BF16)
            ubB = Buf("ub")
            cacc = carve([S], F32)
            caB = Buf("cacc")
            oflat = carve([2 * 16 * 130], F32)
            outb = oflat.rearrange("p (a b c) -> p a b c", a=2, b=16)
            gpos = [0]

            def gsub(shape):
                n = int(np.prod(shape))
                v = oflat[:, gpos[0]:gpos[0] + n]
                gpos[0] += n
                assert gpos[0] <= 2 * 16 * 130
                if len(shape) == 2:
                    v = v.rearrange("p (a b) -> p a b", a=shape[0])
                elif len(shape) == 3:
                    v = v.rearrange("p (a b c) -> p a b c", a=shape[0], b=shape[1])
                return v

            ga = gsub([16, 16])
            t1 = gsub([16, 16])
            t2 = gsub([16, 16])
            lfa = gsub([16, 16])
            u = gsub([2, 16, 4])
            bsb = gsub([2, 16, 4])
            btot = gsub([2, 16, 4])
            umx = gsub([2, 16, 4])
            mref = gsub([2, 16, 4])
            min_ = gsub([2, 17, 4])
            diag = gsub([128])
            umc = gsub([2])
            oB = [Buf("outb0"), Buf("outb1")]
            hacc = carve([16, 128], F32)
            haB = Buf("hacc")
            hnb = carve([16, 128], BF16)
            hnB = Buf("hnb")
            NAT = 6
            ATR = Rot([(carve([128], BF16), Buf(f"AT{i}")) for i in range(NAT)])
            vER = Rot([(carve([130], BF16), Buf(f"vE{i}")) for i in range(NAT)])
            CdR = Rot([(carve([130], BF16), Buf(f"Cd{i}")) for i in range(4)])
            Caug = [carve([130], F32) for d in range(2)]
            CaB = [Buf("Caug0"), Buf("Caug1")]
            sml = carve([8, 16], F32)
            smB = Buf("sml")
            sgo = carve([512], F32)
            sgoB = Buf("sgo")
            for st in HS:
                c.op("dve", lambda e, st=st: e.memset(st["vaug"][:, :, 128:130], 1.0), writes=[st["vB"]])

            def gates_gen():
                psGt = Rot([(psum[i], psB[i]) for i in range(6)])
                wg_s, wgB = wload(wmR, wing_d[l])
                pg, pgB = psGt.next()
                for ch in range(16):
                    for kc in range(8):
                        c.mm(pg[:, ch * 16:(ch + 1) * 16], hTm[:, kc, ch * 128:(ch + 1) * 128], wg_s[:, kc * 16:(kc + 1) * 16],
                             start=(kc == 0), stop=(kc == 7), reads=[hTmB[ch // 4], wgB], writes=[pgB], inc=(kc == 7 and ch == 15))
                gb = PARs("gbias", l * 16, 16).unsqueeze(1).broadcast_to([128, 16, 16])
                pg3 = pg[:, 0:256].rearrange("p (a b) -> p a b", a=16)
                c.tt("dve", ga, pg3, gb, ALU.add, reads=[pgB, cstB], writes=[gB])
                c.stt(t1, ga, -1.0, ga, ALU.mult, ALU.max, reads=[gB], writes=[gB])
                c.act(t1, t1, AF.Exp, scale=-1.0, reads=[gB], writes=[gB])
                c.act(t1, t1, AF.Ln, bias=CFs("one"), reads=[gB, cstB], writes=[gB])
                c.ts("dve", t2, ga, 0.0, None, ALU.min, reads=[gB], writes=[gB])
                c.tt("dve", lfa, t2, t1, ALU.subtract, reads=[gB], writes=[gB])
                yield
                lfa2 = lfa.rearrange("p a b -> p (a b)")
                pcs = []
                for nm in ("LT", "UT", "ones"):
                    p_, pB_ = psGt.next()
                    c.mm(p_[:, 0:256], CFs(nm), lfa2, start=True, stop=True, reads=[gB, cstB], writes=[pB_])
                    pcs.append((p_[:, 0:256].rearrange("p (a b) -> p a b", a=16), pB_))
                for d in range(2):
                    fo = 4 + 8 * d
                    io = 8 * d
                    c.copy("dve", bsb[:, d], pcs[d][0][:, :, fo:fo + 4], reads=[pcs[d][1]], writes=[gB])
                    c.copy("dve", btot[:, d], pcs[2][0][:, :, fo:fo + 4], reads=[pcs[2][1]], writes=[gB])
                    c.tt("dve", u[:, d], ga[:, :, io:io + 4], bsb[:, d], ALU.subtract, reads=[gB], writes=[gB])
                yield
                u2 = u.rearrange("p a b c -> p (a b c)")
                pt_, ptB = psGt.next()
                c.tr(pt_[:, 0:128], u2, CFs("ident"), reads=[gB, cstB], writes=[ptB])
                c.red(umc[:, 0:1], pt_[:, 0:128], ALU.max, reads=[ptB], writes=[gB])
                c.ts("dve", diag, CFs("ident"), umc[:, 0:1], None, ALU.mult, reads=[gB, cstB], writes=[gB])
                pr_, prB = psGt.next()
                c.mm(pr_[:, 0:128], CFs("ones"), diag, start=True, stop=True, reads=[gB, cstB], writes=[prB])
                c.copy("dve", umx.rearrange("p a b c -> p (a b c)"), pr_[:, 0:128], reads=[prB], writes=[gB])
                yield
                c.op("dve", lambda e: e.memset(min_[:, 0, 0, :], 0.0), writes=[gB])
                c.op("dve", lambda e: e.memset(min_[:, 1, 16, :], 0.0), writes=[gB])
                for ch in range(16):
                    c.tt("dve", mref[:, 0, ch, :], min_[:, 0, ch, :], umx[:, 0, ch, :], ALU.max, reads=[gB], writes=[gB])
                    c.tt("dve", min_[:, 0, ch + 1, :], btot[:, 0, ch, :], mref[:, 0, ch, :], ALU.add, reads=[gB], writes=[gB])
                    if ch % 4 == 0:
                        yield
                for ch in range(15, -1, -1):
                    c.tt("dve", mref[:, 1, ch, :], min_[:, 1, ch + 1, :], umx[:, 1, ch, :], ALU.max, reads=[gB], writes=[gB])
                    c.tt("dve", min_[:, 1, ch, :], btot[:, 1, ch, :], mref[:, 1, ch, :], ALU.add, reads=[gB], writes=[gB])
                    if ch % 4 == 0:
                        yield
                c.tt("dve", dcy[:, 0], min_[:, 0, 0:16, :], mref[:, 0], ALU.subtract, reads=[gB], writes=[gB])
                c.tt("dve", dcy[:, 1], min_[:, 1, 1:17, :], mref[:, 1], ALU.subtract, reads=[gB], writes=[gB])
                c.act(dcy, dcy, AF.Exp, reads=[gB], writes=[gB])
                c.tt("dve", Eg, u, mref, ALU.subtract, reads=[gB], writes=[gB])
                c.act(Eg, Eg, AF.Exp, reads=[gB], writes=[gB])
                c.tt("dve", flo, bsb, mref, ALU.add, reads=[gB], writes=[gB])
                c.act(flo, flo, AF.Exp, scale=-1.0, reads=[gB], writes=[gB])


                yield

            psU = Rot([(psum[6], psB[6]), (psum[7], psB[7])])

            def prep(h):
                st = HS[h % 2]
                qT, kT, k_tm, vaug = st["qT"], st["kT"], st["k_tm"], st["vaug"]
                for t_, dstT, dB in ((0, qT, st["qB"]), (1, kT, st["kB"])):
                    ws, wB = wload(wmR, winb_d[l, 6 + 4 * h + t_])
                    for tt in range(4):
                        tok = slice(tt * 512, (tt + 1) * 512)
                        p_, pB_ = psU.next()
                        for kc in range(8):
                            c.mm(p_[:, :], ws[:, kc * 128:(kc + 1) * 128], hTm[:, kc, tok], start=(kc == 0), stop=(kc == 7),
                                 reads=[wB, hTmB[tt]], writes=[pB_])
                        c.copy("act", ub[:, tok], p_[:, :], reads=[pB_], writes=[ubB])
                        yield
                    blk = t_ * 4 + h
                    w0 = PARs("convw", (l * 3 + 0) * 8 + blk)
                    w1 = PARs("convw", (l * 3 + 1) * 8 + blk)
                    w2 = PARs("convw", (l * 3 + 2) * 8 + blk)
                    b_ = PARs("convb", l * 8 + blk)
                    c.act(cacc, ub, AF.Identity, scale=w1, bias=b_, reads=[ubB, cstB], writes=[caB])
                    yield
                    c.stt(cacc[:, 1:S], ub[:, 0:S - 1], w0, cacc[:, 1:S], ALU.mult, ALU.add, reads=[ubB, cstB], writes=[caB])
                    yield
                    c.stt(cacc[:, 0:S - 1], ub[:, 1:S], w2, cacc[:, 0:S - 1], ALU.mult, ALU.add, reads=[ubB, cstB], writes=[caB])
                    c.act(dstT, cacc, AF.Silu, reads=[caB], writes=[dB])
                    yield
                ws, wB = wload(wmR, winb_d[l, 6 + 4 * h + 2])
                for c4 in range(4):
                    p_, pB_ = psU.next()
                    for k in range(4):
                        ch = c4 * 4 + k
                        for kc in range(8):
                            c.mm(p_[:, k * 128:(k + 1) * 128], hTm[:, kc, ch * 128:(ch + 1) * 128], ws[:, kc * 128:(kc + 1) * 128],
                                 start=(kc == 0), stop=(kc == 7), reads=[wB, hTmB[c4]], writes=[pB_], inc=(kc == 7 and k == 3))
                    c.copy("act", vaug[:, c4 * 4:(c4 + 1) * 4, 0:128], p_[:, :].rearrange("p (a b) -> p a b", a=4),
                           reads=[pB_], writes=[st["vB"]])
                    yield
                for c4 in range(4):
                    p_, pB_ = psU.next()
                    pbf = p_[:, :].bitcast(BF16)
                    for k in range(4):
                        ch = c4 * 4 + k
                        c.tr(pbf[:, k * 128:(k + 1) * 128], kT[:, ch * 128:(ch + 1) * 128], CBs("ident"),
                             reads=[st["kB"], cstB], writes=[pB_], inc=(k == 3))
                    c.copy("dve", k_tm[:, c4 * 4:(c4 + 1) * 4, :], pbf[:, 0:512].rearrange("p (a b) -> p a b", a=4),
                           reads=[pB_], writes=[st["ktB"]])
                    yield

            def bank_pair_slots(banks, width):
                items = []
                for b_ in banks:
                    bb = Buf(f"psbank{b_}")
                    for q in range(2):
                        items.append((psum[b_][:, q * 256:q * 256 + width], bb))
                return Rot(items)

            psSl = Rot([(psum[b_][:, 0:128], psB[b_]) for b_ in (0, 1)])
            psGl = Rot([(psum[b_][:, 0:129], psB[b_]) for b_ in (2, 3)])
            psOl = Rot([(psum[b_][:, 0:129], psB[b_]) for b_ in (4, 5)])

            def stageA(h, d, i):
                st = HS[h % 2]
                ch = i if d == 0 else 15 - i
                cs = slice(ch * 128, (ch + 1) * 128)
                Ecol = Eg[:, d, ch, h:h + 1]
                pS, pSB = psSl.next()
                c.mm(pS, st["kT"][:, cs], st["qT"][:, cs], start=True, stop=True, reads=[st["qB"], st["kB"]], writes=[pSB])
                at, atB = ATR.next()
                c.stt(at, pS, Ecol, CBs("maskF" if d == 0 else "maskB"), ALU.mult, ALU.mult,
                      reads=[pSB, gB, cstB], writes=[atB])
                ve, veB = vER.next()
                c.ts("pool", ve, st["vaug"][:, ch, :], Ecol, 1.0, ALU.mult, ALU.mult, reads=[st["vB"], gB], writes=[veB])
                return (at, atB, ve, veB)

            def emit_cd(h, d, i):
                ch = i if d == 0 else 15 - i
                dcol = dcy[:, d, ch, h:h + 1]
                cd, cdB = CdR.next()
                c.act(cd[:, 0:129], Caug[d][:, 0:129], AF.Copy, scale=dcol, reads=[CaB[d], gB], writes=[cdB])
                return (cd, cdB)

            def emit_G(h, d, i, a_):
                st = HS[h % 2]
                at, atB, ve, veB = a_
                ch = i if d == 0 else 15 - i
                pG, pGB = psGl.next()
                c.mm(pG, st["k_tm"][:, ch, :], ve[:, 0:129], start=True, stop=True, reads=[st["ktB"], veB], writes=[pGB])
                return (pG, pGB)

            def emit_out(h, d, i, a_, cd_):
                st = HS[h % 2]
                at, atB, ve, veB = a_
                ch = i if d == 0 else 15 - i
                cs = slice(ch * 128, (ch + 1) * 128)
                vaug = st["vaug"]
                pO, pOB = psOl.next()
                if cd_ is not None:
                    cd, cdB = cd_
                    c.mm(pO, at, vaug[:, ch, 0:129], start=True, stop=False, reads=[atB, st["vB"]], writes=[pOB])
                    c.mm(pO, st["qT"][:, cs], cd[:, 0:129], start=False, stop=True, reads=[st["qB"], cdB], writes=[pOB])
                else:
                    c.mm(pO, at, vaug[:, ch, 0:129], start=True, stop=True, reads=[atB, st["vB"]], writes=[pOB])
                return (pO, pOB)

            def emit_stt(h, d, i, g_):
                ch = i if d == 0 else 15 - i
                dcol = dcy[:, d, ch, h:h + 1]
                pG, pGB = g_
                if i == 0:
                    c.copy("dve", Caug[d][:, 0:129], pG, reads=[pGB], writes=[CaB[d]])
                elif i < 15:
                    c.stt(Caug[d][:, 0:129], Caug[d][:, 0:129], dcol, pG, ALU.mult, ALU.add,
                          reads=[pGB, gB], writes=[CaB[d]])

            ocp = [0]

            def emit_ocopy(d, i, o_):
                ch = i if d == 0 else 15 - i
                pO, pOB = o_
                ocp[0] += 1
                c.copy("act" if ocp[0] % 2 else "dve", outb[:, d, ch, 0:129], pO, reads=[pOB], writes=[oB[d]])

            def epi1(h):
                for d in range(2):
                    dn = sml[:, d, :]
                    c.ts("dve", dn, outb[:, d, :, 128], CMUL, None, ALU.mult, reads=[oB[d]], writes=[smB])
                    c.stt(dn, dn, -1.0, dn, ALU.mult, ALU.max, reads=[smB], writes=[smB])
                    c.tt("dve", dn, dn, flo[:, d, :, h], ALU.max, reads=[gB], writes=[smB])
                    c.recip(dn, dn, reads=[smB], writes=[smB])
                    c.ts("dve", dn, dn, CMUL, None, ALU.mult, reads=[smB], writes=[smB])
                c.tt("dve", hacc, outb[:, 0, :, 0:128], sml[:, 0, :].unsqueeze(2).broadcast_to([128, 16, 128]), ALU.mult,
                     reads=[smB, oB[0]], writes=[haB])
                c.tt("dve", outb[:, 1, :, 0:128], outb[:, 1, :, 0:128], sml[:, 1, :].unsqueeze(2).broadcast_to([128, 16, 128]),
                     ALU.mult, reads=[smB], writes=[oB[1]])
                c.tt("dve", hacc, hacc, outb[:, 1, :, 0:128], ALU.add, reads=[oB[1]], writes=[haB])

            def epi2(h):
                s1 = sml[:, 2, :]
                s2 = sml[:, 3, :]
                mu = sml[:, 4, :]
                var = sml[:, 5, :]
                nmr = sml[:, 6, :]
                c.red(s1, hacc, ALU.add, reads=[haB], writes=[smB])
                yield
                for c4 in range(4):
                    for k in range(4):
                        ch = c4 * 4 + k
                        c.act(hnb[:, ch, :], hacc[:, ch, :], AF.Square, accum_out=s2[:, ch:ch + 1], reads=[haB], writes=[hnB, smB])
                    yield
                c.ts("dve", mu, s1, 1.0 / 128, None, ALU.mult, reads=[smB], writes=[smB])
                c.tt("dve", var, mu, mu, ALU.mult, reads=[smB], writes=[smB])
                c.stt(var, s2, 1.0 / 128, var, ALU.mult, ALU.subtract, reads=[smB], writes=[smB])
                c.act(var, var, AF.Ln, bias=CFs("eps"), reads=[smB, cstB], writes=[smB])
                c.act(var, var, AF.Exp, scale=-0.5, reads=[smB], writes=[smB])
                yield
                c.stt(nmr, mu, -1.0, var, ALU.mult, ALU.mult, reads=[smB], writes=[smB])
                yield
                for c4 in range(4):
                    for k in range(4):
                        ch = c4 * 4 + k
                        c.act(hnb[:, ch, :], hacc[:, ch, :], AF.Identity, scale=var[:, ch:ch + 1], bias=nmr[:, ch:ch + 1],
                              reads=[haB, smB], writes=[hnB])
                    yield
                ws, wB = wload(wmR, winb_d[l, 6 + 4 * h + 3])
                for tt in range(4):
                    tok = slice(tt * 512, (tt + 1) * 512)
                    p_, pB_ = psU.next()
                    for kc in range(8):
                        c.mm(p_[:, :], ws[:, kc * 128:(kc + 1) * 128], hTm[:, kc, tok], start=(kc == 0), stop=(kc == 7),
                             reads=[wB, hTmB[tt]], writes=[pB_])
                    c.act(sgo, p_[:, :], AF.Tanh, scale=0.5, reads=[pB_], writes=[sgoB])
                    c.ts("dve", sgo, sgo, 0.5, 0.5, ALU.mult, ALU.add, reads=[sgoB], writes=[sgoB])
                    p2, p2B = psU.next()
                    pbf = p2[:, :].bitcast(BF16)
                    for k in range(4):
                        ch = tt * 4 + k
                        c.tr(pbf[:, k * 128:(k + 1) * 128], hnb[:, ch, :], CBs("ident"), reads=[hnB, cstB], writes=[p2B], inc=(k == 3))
                    c.stt(y_mT[:, h, tok], pbf[:, 0:512], PARs("gmn", l * 4 + h), sgo, ALU.mult, ALU.mult,
                          reads=[p2B, sgoB, cstB], writes=[ymB[tt]])
                    yield

            def exhaust(g):
                for _ in g:
                    pass

            def pump(gens, n=1):
                for g in gens:
                    for _ in range(n):
                        try:
                            next(g)
                        except StopIteration:
                            break

            LA = 2
            g1, g2 = gates_gen(), prep(0)
            alive = [g1, g2]
            while alive:
                for g in list(alive):
                    try:
                        next(g)
                    except StopIteration:
                        alive.remove(g)
            for d in range(2):
                oB[d].readers.extend(gB.writers + gB.readers)
            pend_epi = None
            for h in range(4):
                gens = []
                if pend_epi is not None:
                    gens.append(pend_epi)
                if h + 1 < 4:
                    gens.append(prep(h + 1))
                As = {}
                for i in range(LA):
                    for d in range(2):
                        As[(d, i)] = stageA(h, d, i)
                cds = {(0, 0): None, (1, 0): None}
                outs = {}
                for k in range(16):
                    if k + LA < 16:
                        for d in range(2):
                            As[(d, k + LA)] = stageA(h, d, k + LA)
                    if k > 0:
                        for d in range(2):
                            emit_ocopy(d, k - 1, outs.pop((d, k - 1)))
                    pump(gens, 1)
                    gs = {}
                    for d in range(2):
                        gs[d] = emit_G(h, d, k, As[(d, k)])
                    for d in range(2):
                        outs[(d, k)] = emit_out(h, d, k, As[(d, k)], cds.pop((d, k)))
                    for d in range(2):
                        emit_stt(h, d, k, gs[d])
                        As.pop((d, k))
                    if k + 1 < 16:
                        for d in range(2):
                            cds[(d, k + 1)] = emit_cd(h, d, k + 1)
                for d in range(2):
                    emit_ocopy(d, 15, outs.pop((d, 15)))
                for g in gens:
                    exhaust(g)
                epi1(h)
                pend_epi = epi2(h)
            exhaust(pend_epi)

        def attention(l, seq, hTm, hTmB, y_aT, yaB, wmR):
            Ct = carve([S], BF16)
            St = carve([S], BF16)
            csB = Buf("cs")
            mark_after_ya[0] = apos[0]
            mark2 = apos[0]
            posi = carve([S], I32)
            ang = carve([S], F32)
            tB = Buf("tbl")
            c.dma("sp", "pos", posi, pos_d[seq], writes=[tB])
            c.copy("dve", ang, posi, reads=[tB], writes=[tB])
            c.ts("dve", ang, ang, CFs("invf"), float(1.0 / (2 * np.pi)), ALU.mult, ALU.mult, reads=[tB, cstB], writes=[tB])
            posf = posi.bitcast(F32)
            tmp = carve([S], F32)
            for (shift, dst) in ((0.0, St), (0.25, Ct)):
                c.ts("dve", posi, ang, float(shift), None, ALU.add, reads=[tB], writes=[tB])
                c.copy("dve", tmp, posi, reads=[tB], writes=[tB])
                c.stt(tmp, ang, float(shift), tmp, ALU.add, ALU.subtract, reads=[tB], writes=[tB])
                c.ts("dve", posf, tmp, 0.5, None, ALU.is_ge, reads=[tB], writes=[tB])
                c.tt("dve", tmp, tmp, posf, ALU.subtract, reads=[tB], writes=[tB])
                c.act(dst, tmp, AF.Sin, scale=float(2 * np.pi), reads=[tB], writes=[csB])
            phase_reset(mark2)
            qrT = carve([4, S], BF16)
            krT = carve([S], BF16)
            qkB_ = [[Buf(f"qr{i}_{t}") for t in range(4)] for i in range(5)]
            vA = carve([16, 2, 66], BF16)
            vAB = Buf("vA")
            zR = Rot([(carve([512], BF16), carve([512], BF16), Buf(f"z{i}")) for i in range(3)])
            tqR = Rot([(carve([512], F32), carve([512], F32), carve([512], BF16), Buf(f"tq{i}")) for i in range(3)])
            pTs = [carve([512], BF16) for i in range(9)]
            pTR = Rot([(pTs[i], Buf(f"pT{i}")) for i in range(9)])
            yat = [carve([512], BF16) for i in range(2)]
            yatB = [Buf("yat0"), Buf("yat1")]
            es_ = carve([8], F32)
            dn_ = carve([2, 4], F32)
            smB = Buf("asml")
            c.act(es_, PARs("sink", l * 8, 8), AF.Exp, reads=[cstB], writes=[smB])
            c.op("dve", lambda e: e.memset(vA[:, :, :, 64:66], 1.0), writes=[vAB])
            psP1 = Rot([(psum[i], psB[i]) for i in (0, 1, 2)])
            psP2 = Rot([(psum[i], psB[i]) for i in (3, 4)])
            psP3 = Rot([(psum[i], psB[i]) for i in (5, 6, 7)])
            units = [(bi, tt) for bi in range(5) for tt in range(4)]
            wcur = {}
            wcur[0] = wload(wmR, winb_d[l, 0])

            def st1(bi, tt):
                if tt == 0:
                    wcur[bi + 1] = wload(wmR, winb_d[l, bi + 1])
                ws, wB = wcur[bi]
                gcol = PARs("gq" if bi < 4 else "gk", l)
                tok = slice(tt * 512, (tt + 1) * 512)
                p_, pB_ = psP1.next()
                for kc in range(8):
                    c.mm(p_[:, :], ws[:, kc * 128:(kc + 1) * 128], hTm[:, kc, tok], start=(kc == 0), stop=(kc == 7),
                         reads=[wB, hTmB[tt]], writes=[pB_])
                zsq, zg, zB = zR.next()
                c.act(zsq, p_[:, :], AF.Square, reads=[pB_], writes=[zB])
                c.act(zg, p_[:, :], AF.Copy, scale=gcol, reads=[pB_, cstB], writes=[zB])
                return dict(bi=bi, tt=tt, tok=tok, zsq=zsq, zg=zg, zB=zB)

            def st2(u_):
                lnq, t1, t2, tqB = tqR.next()
                pss, pssB = psP2.next()
                c.mm(pss[:, :], CBs("blk64"), u_["zsq"], start=True, stop=True, reads=[u_["zB"], cstB], writes=[pssB])
                prz, przB = psP3.next()
                c.mm(prz[:, :], CBs("rblk"), u_["zg"], start=True, stop=True, reads=[u_["zB"], cstB], writes=[przB])
                c.act(lnq, pss[:, :], AF.Ln, bias=CFs("eps"), scale=1.0 / 64, reads=[pssB, cstB], writes=[tqB])
                c.act(lnq, lnq, AF.Exp, scale=-0.5, reads=[tqB], writes=[tqB])
                c.tt("pool", t2, u_["zg"], Ct[:, u_["tok"]], ALU.mult, reads=[u_["zB"], csB], writes=[tqB])
                u_.update(rs=lnq, t1=t1, t2=t2, tqB=tqB, prz=prz, przB=przB)

            def st3(u_):
                bi, tt, tok = u_["bi"], u_["tt"], u_["tok"]
                dst = qrT[:, bi, tok] if bi < 4 else krT[:, tok]
                t1, t2, rs, tqB = u_["t1"], u_["t2"], u_["rs"], u_["tqB"]
                c.tt("dve", t1, u_["prz"][:, :], St[:, tok], ALU.mult, reads=[u_["przB"], csB], writes=[tqB])
                c.tt("dve", t1, t1, t2, ALU.add, reads=[tqB], writes=[tqB])
                c.tt("dve", dst, t1, rs, ALU.mult, reads=[tqB], writes=[qkB_[bi][tt]])

            live = []
            for ui in range(len(units) + 2):
                if ui < len(units):
                    live.append(st1(*units[ui]))
                if 1 <= ui <= len(units):
                    st2(live[ui - 1])
                if ui >= 2:
                    st3(live[ui - 2])
            wnext = wcur[5]
            ws, wB = wnext
            for c4 in range(4):
                p_, pB_ = psM.next()
                for k in range(4):
                    ch = c4 * 4 + k
                    for kc in range(8):
                        c.mm(p_[:, k * 128:(k + 1) * 128], hTm[:, kc, ch * 128:(ch + 1) * 128], ws[:, kc * 128:(kc + 1) * 128],
                             start=(kc == 0), stop=(kc == 7), reads=[wB, hTmB[c4]], writes=[pB_], inc=(kc == 7 and k == 3))
                c.copy("act", vA[:, c4 * 4:(c4 + 1) * 4, :, 0:64],
                       p_[:, :].rearrange("p (a b d) -> p a b d", a=4, b=2), reads=[pB_], writes=[vAB])
            iters = [(n, kvh) for n in range(16) for kvh in range(2)]

            def emit_S(n, kvh):
                qs = slice(n * 128, (n + 1) * 128)
                pr = slice(kvh * 64, kvh * 64 + 64)
                js = [j for j in (n - 1, n, n + 1) if 0 <= j < 16]
                pts = []
                for j in js:
                    ks = slice(j * 128, (j + 1) * 128)
                    ps_, psB_ = psSa.next()
                    rd = [qkB_[i][n // 4] for i in range(4)] + [qkB_[4][j // 4]]
                    c.mm(ps_[:, :], krT[pr, ks], qrT[pr, :, qs], start=True, stop=(j == n), reads=rd, writes=[psB_])
                    if j != n:
                        c.mm(ps_[:, :], CBs("ident"), CBs("mprev" if j < n else "mnext"), start=False, stop=True,
                             reads=[cstB], writes=[psB_])
                    pt, ptB_ = pTR.next()
                    c.act(pt, ps_[:, :], AF.Exp, scale=0.125, reads=[psB_], writes=[ptB_])
                    pts.append((j, pt, ptB_))
                return pts

            def emit_PV(n, kvh, pts):
                ya, yB_ = yat[n % 2], yatB[n % 2]
                po, poB = psOa.next()
                po3 = po[:, :].rearrange("p (g d) -> p g d", g=4)
                for g in range(4):
                    for idx, (j, pt, ptB_) in enumerate(pts):
                        c.mm(po3[:, g, 0:65], pt[:, g * 128:(g + 1) * 128], vA[:, j, kvh, 0:65], start=(idx == 0),
                             stop=(idx == len(pts) - 1), reads=[ptB_, vAB], writes=[poB],
                             inc=(g == 3 and idx == len(pts) - 1))
                dn = dn_[:, kvh, :]
                c.tt("dve", dn, po3[:, :, 64], es_[:, kvh * 4:(kvh + 1) * 4], ALU.add, reads=[poB, smB], writes=[smB])
                c.recip(dn, dn, reads=[smB], writes=[smB])
                c.tt("dve", ya[:, kvh * 256:(kvh + 1) * 256].rearrange("p (g d) -> p g d", g=4), po3[:, :, 0:64],
                     dn.unsqueeze(2).broadcast_to([128, 4, 64]), ALU.mult, reads=[poB, smB], writes=[yB_])
                return n if kvh == 1 else None

            def emit_TR(n):
                ya, yB_ = yat[n % 2], yatB[n % 2]
                if True:
                    qs = slice(n * 128, (n + 1) * 128)
                    p2, p2B = psTa.next()
                    pbf = p2[:, :].bitcast(BF16)
                    for k in range(4):
                        c.tr(pbf[:, k * 128:(k + 1) * 128], ya[:, k * 128:(k + 1) * 128], CBs("ident"), reads=[yB_, cstB],
                             writes=[p2B], inc=(k == 3))
                    c.copy("act", y_aT[:, :, qs], pbf[:, 0:512].rearrange("p (a b) -> p a b", a=4), reads=[p2B], writes=[yaB[n // 4]])

            psSa = Rot([(psum[i], psB[i]) for i in (0, 1, 2, 3, 4)])
            psOa = Rot([(psum[i], psB[i]) for i in (5, 6)])
            psTa = Rot([(psum[7], psB[7])])
            prev = None
            pend_tr = None
            for (n, kvh) in iters:
                pts = emit_S(n, kvh)
                if pend_tr is not None:
                    emit_TR(pend_tr)
                    pend_tr = None
                if prev is not None:
                    pend_tr = emit_PV(*prev)
                prev = (n, kvh, pts)
            r_ = emit_PV(*prev)
            if pend_tr is not None:
                emit_TR(pend_tr)
            if r_ is not None:
                emit_TR(r_)

        def merge(l, hTm, hTmB, y_aT, yaB, y_mT, ymB, wmR, mix_attn, mix_mlstm):
            TM = 1024
            nsub = TM // 512
            tR = Rot([(carve([512], F32), Buf(f"mt{i}")) for i in range(4)])
            aR = Rot([(carve([512], F32), Buf(f"ma{i}")) for i in range(4)])
            mgT = carve([8, TM], BF16)
            mgB = [Buf(f"mg{m}") for m in range(8)]
            for tt in range(S // TM):
                for m in range(8):
                    parts = [[] for _ in range(nsub)]
                    for (on, yT, yB, wsrc, gblk) in ((mix_attn, y_aT, yaB, wba_d, 22 + m),
                                                     (mix_mlstm, y_mT, ymB, wbm_d, 30 + m)):
                        if not on:
                            continue
                        ws, wB = wload(wmR, wsrc[l, m])
                        ws2, wB2 = wload(wmR, winb_d[l, gblk])
                        for sub in range(nsub):
                            t5 = tt * nsub + sub
                            tok = slice(t5 * 512, (t5 + 1) * 512)
                            pb_, pbB = psM.next()
                            for kc in range(4):
                                c.mm(pb_[:, :], ws[:, kc * 128:(kc + 1) * 128], yT[:, kc, tok], start=(kc == 0), stop=(kc == 3),
                                     reads=[wB, yB[t5]], writes=[pbB])
                            pg_, pgB_ = psM.next()
                            for kc in range(8):
                                c.mm(pg_[:, :], ws2[:, kc * 128:(kc + 1) * 128], hTm[:, kc, tok], start=(kc == 0), stop=(kc == 7),
                                     reads=[wB2, hTmB[t5]], writes=[pgB_])
                            tb, tbB = tR.next()
                            ab, abB = aR.next()
                            c.act(tb, pg_[:, :], AF.Tanh, scale=0.5, reads=[pgB_], writes=[tbB])
                            c.stt(ab, tb, 1.0, pb_[:, :], ALU.add, ALU.mult, reads=[pbB, tbB], writes=[abB])
                            parts[sub].append((ab, abB))
                    for sub in range(nsub):
                        ss_ = slice(sub * 512, (sub + 1) * 512)
                        if len(parts[sub]) == 2:
                            c.tt("dve", mgT[:, m, ss_], parts[sub][0][0], parts[sub][1][0], ALU.add,
                                 reads=[parts[sub][0][1], parts[sub][1][1]], writes=[mgB[m]])
                        else:
                            c.copy("dve", mgT[:, m, ss_], parts[sub][0][0], reads=[parts[sub][0][1]], writes=[mgB[m]])
                for m2 in range(8):
                    ws, wB = wload(wmR, wo_d[l, m2])
                    for sub in range(nsub):
                        ss_ = slice(sub * 512, (sub + 1) * 512)
                        t5 = tt * nsub + sub
                        tok = slice(t5 * 512, (t5 + 1) * 512)
                        po_, poB_ = psM.next()
                        for m in range(8):
                            c.mm(po_[:, :], ws[:, m * 128:(m + 1) * 128], mgT[:, m, ss_], start=(m == 0), stop=(m == 7),
                                 reads=[wB, mgB[m]], writes=[poB_])
                        c.stt(xT[:, m2, tok], po_[:, :], 0.5, xT[:, m2, tok], ALU.mult, ALU.add, reads=[poB_], writes=[xB[m2][t5]])

        outBs = []
        for seq in range(nseq):
            load_x(seq)
            for l in layers:
                if do_ffn1:
                    ffn(l, 1)
                if do_mixer:
                    mixer(l, seq, mix_attn, mix_mlstm)
                if do_ffn2:
                    ffn(l, 2)
                if do_norm:
                    final_norm(l)
            outBs.append(store_x(seq))
        c.final_wait("sp", outBs)
        c.emit()
    return nc


def make_in_maps(inputs, nseq=SEQ_PER_CORE, ncores=NCORES):
    w = prep_weights(inputs)
    cbv, cfv = make_consts()
    x = np.ascontiguousarray(inputs["x"], dtype=np.float32)
    pos = np.ascontiguousarray(inputs["positions"], dtype=np.int32)
    in_maps = []
    for ci in range(ncores):
        m = dict(w)
        m["x"] = x[ci * nseq:(ci + 1) * nseq]
        m["pos"] = np.ascontiguousarray(np.broadcast_to(pos[ci * nseq:(ci + 1) * nseq, None, :], (nseq, 128, S)))
        m["cb"] = cbv
        m["cf"] = cfv
        in_maps.append(m)
    return in_maps


def kernel(**inputs):
    nc = build()
    in_maps = make_in_maps(inputs)
    res = run_bass_kernel_spmd(nc, in_maps, core_ids=list(range(NCORES)))
    return np.concatenate([r["out"] for r in res.results], axis=0).astype(np.float32)
```

```python
import contextlib
import numpy as np
import ml_dtypes
import concourse.bass as bass
import concourse.mybir as mybir
from concourse.bass_utils import run_bass_kernel_spmd

F32 = mybir.dt.float32
BF16 = mybir.dt.bfloat16
I32 = mybir.dt.int32
AF = mybir.ActivationFunctionType
ALU = mybir.AluOpType
AX = mybir.AxisListType

ENGS = ("pe", "act", "dve", "pool", "sp")

D = 1024
S = 2048
L = 2
DFF = 2816
NJ = DFF // 128
NCORES = 8
SEQ_PER_CORE = 2
EPS = 1e-6
NEG = -30000.0


class Buf:
    __slots__ = ("name", "writers", "readers")

    def __init__(self, name=""):
        self.name = name
        self.writers = []
        self.readers = []


class Ctx:
    def __init__(self, nc, es):
        self.nc = nc
        self.es = es
        self.streams = {e: [] for e in ENGS}
        self.cnt = {e: 0 for e in ENGS}
        self.sem = {}
        for e in ENGS:
            self.sem[e] = es.enter_context(nc.semaphore("sem_" + e))
        self.waited = {e: {} for e in ENGS}
        self.dma_sems = {}
        self.dma_cnt = {}

    def sb(self, name, shape, dt):
        return self.es.enter_context(self.nc.sbuf_tensor(name, list(shape), dt))

    def ps(self, name, shape, dt):
        return self.es.enter_context(self.nc.psum_tensor(name, list(shape), dt))

    def dma_sem(self, key):
        if key not in self.dma_sems:
            self.dma_sems[key] = self.es.enter_context(self.nc.semaphore("dsem_" + str(key)))
            self.dma_cnt[key] = 0
        return self.dma_sems[key]

    def _deps(self, reads, writes):
        deps = []
        for b in reads:
            deps.extend(b.writers)
        for b in writes:
            deps.extend(b.writers)
            deps.extend(b.readers)
        return deps

    def _waits(self, eng, deps):
        best = {}
        for (k, v) in deps:
            if k == ("eng", "pe") and eng == "pe":
                continue
            if v > best.get(k, 0):
                best[k] = v
        out = []
        for k, v in best.items():
            if self.waited[eng].get(k, 0) >= v:
                continue
            self.waited[eng][k] = v
            sem = self.sem[k[1]] if k[0] == "eng" else self.dma_sems[k[1]]
            out.append((sem, v))
        return out

    def _record(self, ev, reads, writes):
        for b in reads:
            if len(b.readers) > 24:
                best = {}
                for (k, v) in b.readers:
                    if v > best.get(k, 0):
                        best[k] = v
                b.readers = list(best.items())
            b.readers.append(ev)
        for b in writes:
            b.writers = [ev]
            b.readers = []

    def op(self, eng, fn, reads=(), writes=(), inc=True):
        deps = self._deps(reads, writes)
        waits = self._waits(eng, deps)
        if inc:
            self.cnt[eng] += 1
            ev = (("eng", eng), self.cnt[eng])
        else:
            ev = (("eng", eng), self.cnt[eng] + 1)
        sem = self.sem[eng]

        def run(e, fn=fn, waits=waits, inc=inc, sem=sem):
            for (s, v) in waits:
                e.wait_ge(s, v)
            ins = fn(e)
            if inc:
                ins.then_inc(sem, 1)

        self.streams[eng].append(run)
        self._record(ev, reads, writes)
        return ev

    def dma(self, eng, key, out, in_, reads=(), writes=()):
        sem = self.dma_sem(key)
        deps = self._deps(reads, writes)
        waits = self._waits(eng, deps)
        self.dma_cnt[key] += 16
        ev = (("dma", key), self.dma_cnt[key])

        def run(e, waits=waits, sem=sem, out=out, in_=in_):
            for (s, v) in waits:
                e.wait_ge(s, v)
            e.dma_start(out=out, in_=in_).then_inc(sem, 16)

        self.streams[eng].append(run)
        self._record(ev, reads, writes)
        return ev

    def barrier(self):
        targets = [(("eng", e), self.cnt[e]) for e in ENGS if self.cnt[e] > 0]
        targets += [(("dma", k), v) for k, v in self.dma_cnt.items() if v > 0]
        for eng in ENGS:
            waits = self._waits(eng, targets)

            def run(e, waits=waits):
                for (s_, v) in waits:
                    e.wait_ge(s_, v)

            self.streams[eng].append(run)

    def final_wait(self, eng, bufs):
        deps = []
        for b in bufs:
            deps.extend(b.writers)
        waits = self._waits(eng, deps)

        def run(e, waits=waits):
            for (s, v) in waits:
                e.wait_ge(s, v)

        self.streams[eng].append(run)

    def emit(self):
        nc = self.nc
        st = self.streams
        with nc.Block() as block:
            @block.tensor
            def _(e):
                for f in st["pe"]:
                    f(e)

            @block.scalar
            def _(e):
                for f in st["act"]:
                    f(e)

            @block.vector
            def _(e):
                for f in st["dve"]:
                    f(e)

            @block.gpsimd
            def _(e):
                for f in st["pool"]:
                    f(e)

            @block.sync
            def _(e):
                for f in st["sp"]:
                    f(e)

    def mm(self, out, lhsT, rhs, start, stop, reads=(), writes=(), inc=None):
        if inc is None:
            inc = stop
        return self.op("pe", lambda e: e.matmul(out, lhsT, rhs, start=start, stop=stop),
                       reads=reads, writes=writes, inc=inc)

    def tr(self, out, in_, ident, reads=(), writes=(), inc=True):
        return self.op("pe", lambda e: e.transpose(out, in_, ident), reads=reads, writes=writes, inc=inc)


    def act(self, out, in_, func, bias=None, scale=None, accum_out=None, reads=(), writes=()):
        kw = {}
        if bias is not None:
            kw["bias"] = bias
        if scale is not None:
            kw["scale"] = scale
        if accum_out is not None:
            kw["accum_out"] = accum_out
        return self.op("act", lambda e: e.activation(out=out, in_=in_, func=func, **kw), reads=reads, writes=writes)

    def tt(self, eng, out, in0, in1, op, reads=(), writes=()):
        return self.op(eng, lambda e: e.tensor_tensor(out=out, in0=in0, in1=in1, op=op), reads=reads, writes=writes)

    def stt(self, out, in0, scalar, in1, op0, op1, reads=(), writes=()):
        return self.op("dve", lambda e: e.scalar_tensor_tensor(out=out, in0=in0, scalar=scalar, in1=in1,
                                                               op0=op0, op1=op1), reads=reads, writes=writes)

    def ts(self, eng, out, in0, s1, s2, op0, op1=None, reads=(), writes=()):
        if op1 is None:
            return self.op(eng, lambda e: e.tensor_scalar(out=out, in0=in0, scalar1=s1, scalar2=None, op0=op0),
                           reads=reads, writes=writes)
        return self.op(eng, lambda e: e.tensor_scalar(out=out, in0=in0, scalar1=s1, scalar2=s2, op0=op0, op1=op1),
                       reads=reads, writes=writes)

    def copy(self, eng, out, in_, reads=(), writes=()):
        if eng == "act":
            return self.op("act", lambda e: e.activation(out=out, in_=in_, func=AF.Copy), reads=reads, writes=writes)
        return self.op(eng, lambda e: e.tensor_copy(out, in_), reads=reads, writes=writes)

    def red(self, out, in_, op, axis=None, reads=(), writes=()):
        ax = AX.X if axis is None else axis
        return self.op("dve", lambda e: e.tensor_reduce(out=out, in_=in_, axis=ax, op=op), reads=reads, writes=writes)

    def recip(self, out, in_, reads=(), writes=()):
        return self.op("dve", lambda e: e.reciprocal(out, in_), reads=reads, writes=writes)


class Rot:
    def __init__(self, items):
        self.items = items
        self.i = 0

    def next(self):
        it = self.items[self.i % len(self.items)]
        self.i += 1
        return it


CB = {}
_off = 0
for _n, _w in (("ident", 128), ("ones", 128), ("blk64", 128), ("rblk", 128),
               ("mprev", 512), ("mnext", 512), ("maskF", 128), ("maskB", 128)):
    CB[_n] = (_off, _w)
    _off += _w
NCB = _off

CF = {}
_off = 0
for _n, _w in (("ident", 128), ("LT", 128), ("UT", 128), ("ones", 128), ("invf", 1), ("eps", 1),
               ("one", 1), ("pi", 1), ("negpi", 1), ("twopi", 1), ("zero", 1), ("half", 1)):
    CF[_n] = (_off, _w)
    _off += _w
NCF = _off

PAR = {}
_off = 0
for _n, _w in (("g_ffn1", L * 8), ("g_mix", L * 8), ("g_ffn2", L * 8), ("g_out", L * 8),
               ("gq", L), ("gk", L), ("convw", L * 3 * 8), ("convb", L * 8), ("gmn", L * 4),
               ("gbias", L * 16), ("sink", L * 8)):
    PAR[_n] = (_off, _w)
    _off += _w
NPAR = _off


def make_consts():
    cb = np.zeros((128, NCB), np.float32)
    cf = np.zeros((128, NCF), np.float32)
    I = np.eye(128, dtype=np.float32)
    cb[:, CB["ident"][0]:CB["ident"][0] + 128] = I
    cb[:, CB["ones"][0]:CB["ones"][0] + 128] = 1.0
    blk = np.zeros((128, 128), np.float32)
    blk[:64, :64] = 1.0
    blk[64:, 64:] = 1.0
    cb[:, CB["blk64"][0]:CB["blk64"][0] + 128] = blk
    R = np.zeros((128, 128), np.float32)
    for m in range(128):
        d = m % 64
        if d < 8:
            R[m + 8, m] = -1.0
        elif d < 16:
            R[m - 8, m] = 1.0
    cb[:, CB["rblk"][0]:CB["rblk"][0] + 128] = R
    k = np.arange(128)[:, None]
    q = np.arange(128)[None, :]
    mprev = np.where(k >= q, 0.0, NEG).astype(np.float32)
    mnext = np.where(k <= q, 0.0, NEG).astype(np.float32)
    cb[:, CB["mprev"][0]:CB["mprev"][0] + 512] = np.tile(mprev, (1, 4))
    cb[:, CB["mnext"][0]:CB["mnext"][0] + 512] = np.tile(mnext, (1, 4))
    maskF = (k <= q).astype(np.float32)
    maskB = (k >= q).astype(np.float32)
    cb[:, CB["maskF"][0]:CB["maskF"][0] + 128] = maskF
    cb[:, CB["maskB"][0]:CB["maskB"][0] + 128] = maskB
    cf[:, CF["ident"][0]:CF["ident"][0] + 128] = I
    cf[:, CF["LT"][0]:CF["LT"][0] + 128] = maskF
    cf[:, CF["UT"][0]:CF["UT"][0] + 128] = maskB
    cf[:, CF["ones"][0]:CF["ones"][0] + 128] = 1.0
    half = 8
    inv_freq = np.power(np.float32(500000.0), -np.arange(half, dtype=np.float32) * np.float32(2.0 / 16)).astype(np.float32)
    invf = np.zeros(128, np.float32)
    for p in range(128):
        d = p % 64
        if d < 16:
            invf[p] = inv_freq[d % 8]
    cf[:, CF["invf"][0]] = invf
    cf[:, CF["eps"][0]] = EPS
    cf[:, CF["one"][0]] = 1.0
    cf[:, CF["pi"][0]] = np.pi
    cf[:, CF["negpi"][0]] = -np.pi
    cf[:, CF["twopi"][0]] = 2 * np.pi
    cf[:, CF["zero"][0]] = 0.0
    cf[:, CF["half"][0]] = 0.5
    return cb.astype(ml_dtypes.bfloat16), cf


def tile_w(W, KC):
    K, N = W.shape
    assert K == KC * 128 and N % 128 == 0
    return np.ascontiguousarray(
        W.reshape(KC, 128, N // 128, 128).transpose(2, 1, 0, 3).reshape(N // 128, 128, KC * 128))


def fm(v):
    return np.ascontiguousarray(v.reshape(-1, 128).T)


def prep_weights(inp):
    f32 = np.float32
    out = {}
    for f in (1, 2):
        wg = inp[f"ffn{f}_w_gate"].astype(f32, copy=False)
        wu = inp[f"ffn{f}_w_up"].astype(f32, copy=False)
        wd = inp[f"ffn{f}_w_down"].astype(f32, copy=False)
        gu = np.empty((L, NJ, 128, 2, 1024), f32)
        dd = np.empty((L, 8, 128, DFF), f32)
        for l in range(L):
            gu[l, :, :, 0, :] = tile_w(wg[l], 8)
            gu[l, :, :, 1, :] = tile_w(wu[l], 8)
            dd[l] = tile_w(wd[l], NJ)
        out[f"wgu{f}"] = gu.reshape(L, NJ, 128, 2048)
        out[f"wd{f}"] = dd
    w_in = inp["w_in"].astype(f32, copy=False)
    cols = []
    for i in range(4):
        cols.append(np.concatenate([np.arange(i * 64, i * 64 + 64), np.arange((i + 4) * 64, (i + 4) * 64 + 64)]))
    cols.append(np.arange(512, 640))
    cols.append(np.arange(640, 768))
    base = 768
    for h in range(4):
        for t in range(4):
            cols.append(np.arange(base + t * 512 + h * 128, base + t * 512 + h * 128 + 128))
    gbase = 768 + 2048 + 16
    for m in range(16):
        cols.append(np.arange(gbase + m * 128, gbase + m * 128 + 128))
    cols = np.concatenate(cols)
    nb = len(cols) // 128
    winb = np.empty((L, nb, 128, 1024), f32)
    wing = np.empty((L, 128, 8 * 16), f32)
    for l in range(L):
        winb[l] = tile_w(w_in[l][:, cols], 8)
        wg_ = w_in[l][:, 768 + 2048:768 + 2048 + 16]
        wing[l] = wg_.reshape(8, 128, 16).transpose(1, 0, 2).reshape(128, 128)
    out["winb"] = winb
    out["wing"] = wing
    out["wba"] = np.stack([tile_w(inp["w_branch_attn"][l].astype(f32, copy=False), 4) for l in range(L)])
    out["wbm"] = np.stack([tile_w(inp["w_branch_mlstm"][l].astype(f32, copy=False), 4) for l in range(L)])
    out["wo"] = np.stack([tile_w(inp["w_out"][l].astype(f32, copy=False), 8) for l in range(L)])
    par = np.zeros((128, NPAR), f32)

    def put(name, arr):
        o, w = PAR[name]
        arr = np.asarray(arr, f32).reshape(128, -1)
        assert arr.shape[1] == w, (name, arr.shape, w)
        par[:, o:o + w] = arr

    put("g_ffn1", np.concatenate([fm(inp["ffn1_norm"][l]) for l in range(L)], axis=1))
    put("g_mix", np.concatenate([fm(inp["mix_norm"][l]) for l in range(L)], axis=1))
    put("g_ffn2", np.concatenate([fm(inp["ffn2_norm"][l]) for l in range(L)], axis=1))
    put("g_out", np.concatenate([fm(inp["block_out_norm"][l]) for l in range(L)], axis=1))
    put("gq", np.stack([np.tile(inp["attn_q_norm"][l], 2) for l in range(L)], axis=1))
    put("gk", np.stack([np.tile(inp["attn_k_norm"][l], 2) for l in range(L)], axis=1))
    cw = np.zeros((128, L, 3, 8), f32)
    cbias = np.zeros((128, L, 8), f32)
    for l in range(L):
        for j in range(3):
            cw[:, l, j, :] = fm(inp["mlstm_conv_w"][l, j])
        cbias[:, l, :] = fm(inp["mlstm_conv_b"][l])
    put("convw", cw)
    put("convb", cbias)
    put("gmn", np.concatenate([fm(inp["mlstm_out_norm"][l]) for l in range(L)], axis=1))
    put("gbias", np.broadcast_to(inp["mlstm_gate_bias"].reshape(1, L * 16), (128, L * 16)))
    put("sink", np.broadcast_to(inp["attn_sink"].reshape(1, L * 8), (128, L * 8)))
    out["par"] = par
    return out


def build(layers=(0, 1), nseq=SEQ_PER_CORE, do_ffn1=True, do_mixer=True, do_ffn2=True, do_norm=True,
          TF=1024, mix_attn=True, mix_mlstm=True):
    nc = bass.Bass("TRN2", target_bir_lowering=False)
    dram = {}

    def din(name, shape, dt=F32):
        dram[name] = nc.dram_tensor(name, list(shape), dt, kind="ExternalInput").ap()
        return dram[name]

    x_d = din("x", [nseq, S, D])
    pos_d = din("pos", [nseq, 128, S], I32)
    cb_d = din("cb", [128, NCB], BF16)
    cf_d = din("cf", [128, NCF])
    par_d = din("par", [128, NPAR])
    wgu_d = {1: din("wgu1", [L, NJ, 128, 2048]), 2: din("wgu2", [L, NJ, 128, 2048])}
    wd_d = {1: din("wd1", [L, 8, 128, DFF]), 2: din("wd2", [L, 8, 128, DFF])}
    winb_d = din("winb", [L, 38, 128, 1024])
    wing_d = din("wing", [L, 128, 128])
    wba_d = din("wba", [L, 8, 128, 512])
    wbm_d = din("wbm", [L, 8, 128, 512])
    wo_d = din("wo", [L, 8, 128, 1024])
    out_d = nc.dram_tensor("out", [nseq, S, D], F32, kind="ExternalOutput").ap()

    with contextlib.ExitStack() as es:
        c = Ctx(nc, es)
        xT = c.sb("xT", [128, 8, S], F32)
        xB = [[Buf(f"xT{m}_{t}") for t in range(S // 512)] for m in range(8)]
        cb = c.sb("cb_s", [128, NCB], BF16)
        cf = c.sb("cf_s", [128, NCF], F32)
        par = c.sb("par_s", [128, NPAR], F32)
        cstB = Buf("consts")

        def CBs(name):
            o, w = CB[name]
            return cb[:, o:o + w]

        def CFs(name):
            o, w = CF[name]
            return cf[:, o:o + w]

        def PARs(name, i, n=1):
            o, w = PAR[name]
            return par[:, o + i:o + i + n]

        psum = [c.ps(f"ps{i}", [128, 512], F32) for i in range(8)]
        psB = [Buf(f"ps{i}") for i in range(8)]

        c.dma("sp", "cst", cb[:], cb_d[:, :], writes=[cstB])
        c.dma("sp", "cst", cf[:], cf_d[:, :], writes=[cstB])
        c.dma("sp", "cst", par[:], par_d[:, :], writes=[cstB])

        AW = 35200
        arena = c.sb("arena", [128, AW], F32)
        apos = [0]

        def carve(shape, dt):
            n = int(np.prod(shape))
            words = (n + 1) // 2 if dt == BF16 else n
            o = apos[0]
            assert o + words <= AW, ("arena overflow", o, words, AW)
            apos[0] = o + words
            v = arena[:, o:o + words]
            if dt == BF16:
                v = v.bitcast(BF16)[:, 0:n]
            elif dt == I32:
                v = v.bitcast(I32)
            if len(shape) == 2:
                v = v.rearrange("p (a b) -> p a b", a=shape[0])
            elif len(shape) == 3:
                v = v.rearrange("p (a b c) -> p a b c", a=shape[0], b=shape[1])
            return v

        def phase_reset(to=0):
            c.barrier()
            apos[0] = to

        class FFNBufs:
            pass

        NB = {}

        def alloc_norm(nsets=2):
            NB["sets"] = []
            for i in range(nsets):
                NB["sets"].append(dict(sq=carve([8, 512], BF16), sqB=Buf(f"sq{i}"), lnv=carve([512], F32), lnvB=Buf(f"lnv{i}"),
                                       rstd=carve([512], F32), rstdB=Buf(f"rstd{i}")))
            NB["i"] = 0

        def alloc_ffn():
            fb = FFNBufs()
            alloc_norm(1)
            fb.hT = [carve([8, TF], BF16) for i in range(2)]
            fb.hTB = [Buf(f"hT{i}") for i in range(2)]
            fb.actT = carve([NJ, TF], BF16)
            fb.actB = [Buf(f"act{j}") for j in range(NJ)]
            sg = [carve([512], F32) for i in range(2)]
            fb.sgR = Rot([(sg[i], Buf(f"sg{i}")) for i in range(2)])
            NGU = 3
            fb.wguR = Rot([(carve([2048], BF16), Buf(f"wgu_s{i}"), f"wgu{i}") for i in range(NGU)])
            NWD = 2
            fb.wdR = Rot([(carve([DFF], BF16), Buf(f"wd_s{i}"), f"wd{i}") for i in range(NWD)])
            return fb

        def alloc_io():
            return Rot([(carve([D], F32), Buf(f"xst{i}"), f"xst{i}") for i in range(2)])

        psGU = Rot([(psum[i], psB[i]) for i in (0, 1, 2, 3)])
        psDN = Rot([(psum[i], psB[i]) for i in (4, 5)])
        psN = Rot([(psum[i], psB[i]) for i in (6, 7)])
        psIO = Rot([(psum[i], psB[i]) for i in (0, 1, 2, 3)])
        evac_rr = [0]

        def evac_eng():
            evac_rr[0] += 1
            return "act" if evac_rr[0] % 2 else "dve"

        def copy_op(eng, out, in_, reads, writes):
            return c.copy(eng, out, in_, reads=reads, writes=writes)

        def load_x(seq):
            phase_reset()
            xstR = alloc_io()
            for tt in range(S // 128):
                st, stB, key = xstR.next()
                c.dma("sp", key, st[:], x_d[seq, tt * 128:(tt + 1) * 128, :], writes=[stB])
                for half in range(2):
                    ps, pB = psIO.next()
                    for k in range(4):
                        kc = half * 4 + k
                        c.tr(ps[:, k * 128:(k + 1) * 128], st[:, kc * 128:(kc + 1) * 128], CFs("ident"),
                             reads=[stB, cstB], writes=[pB], inc=(k == 3))
                    dst = xT[:, half * 4:half * 4 + 4, tt * 128:(tt + 1) * 128]
                    src = ps[:, :].rearrange("p (k t) -> p k t", k=4)
                    copy_op(evac_eng(), dst, src, [pB], [xB[half * 4 + k][tt // 4] for k in range(4)])

        def store_x(seq):
            phase_reset()
            xstR = alloc_io()
            outB = Buf("outdram")
            for tt in range(S // 128):
                st, stB, key = xstR.next()
                for half in range(2):
                    ps, pB = psIO.next()
                    for k in range(4):
                        kc = half * 4 + k
                        c.tr(ps[:, k * 128:(k + 1) * 128], xT[:, kc, tt * 128:(tt + 1) * 128], CFs("ident"),
                             reads=[xB[kc][tt // 4], cstB], writes=[pB], inc=(k == 3))
                    copy_op(evac_eng(), st[:, half * 512:(half + 1) * 512], ps[:, :], [pB], [stB])
                c.dma("sp", key + "o", out_d[seq, tt * 128:(tt + 1) * 128, :], st[:], reads=[stB], writes=[outB])
            return outB

        def rms_sq(tt, st=None):
            if st is None:
                st = NB["sets"][NB["i"] % len(NB["sets"])]
                NB["i"] += 1
            NB["cur"] = st
            tok = slice(tt * 512, (tt + 1) * 512)
            for kc in range(8):
                c.act(st["sq"][:, kc, :], xT[:, kc, tok], AF.Square, reads=[xB[kc][tt]], writes=[st["sqB"]])

        def rms_mm(tt, st=None):
            if st is None:
                st = NB["cur"]
            sq, sqB, lnv, lnvB, rstd, rstdB = st["sq"], st["sqB"], st["lnv"], st["lnvB"], st["rstd"], st["rstdB"]
            ps, pB = psN.next()
            for kc in range(8):
                c.mm(ps[:, :], CBs("ones"), sq[:, kc, :], start=(kc == 0), stop=(kc == 7),
                     reads=[sqB, cstB], writes=[pB])
            c.act(lnv, ps[:, :], AF.Ln, bias=CFs("eps"), scale=1.0 / D, reads=[pB, cstB], writes=[lnvB])
            c.act(rstd, lnv, AF.Exp, scale=-0.5, reads=[lnvB], writes=[rstdB])

        def rms_stats(tt):
            rms_sq(tt)
            rms_mm(tt)

        def rms_apply(tt, gname, l, dst_fn, dstB_fn, st=None):
            if st is None:
                st = NB["cur"]
            rstd, rstdB = st["rstd"], st["rstdB"]
            tok = slice(tt * 512, (tt + 1) * 512)
            for kc in range(8):
                g = PARs(gname, l * 8 + kc)
                c.stt(dst_fn(kc), xT[:, kc, tok], g, rstd, ALU.mult, ALU.mult,
                      reads=[xB[kc][tt], rstdB, cstB], writes=dstB_fn(kc))

        def norm_pipe(nt, gname, l, dst_fn, dstB_fn):
            sets = NB["sets"]
            assert len(sets) == 2
            rms_sq(0, sets[0])
            if nt > 1:
                rms_sq(1, sets[1])
            rms_mm(0, sets[0])
            for t in range(nt):
                if t + 1 < nt:
                    rms_mm(t + 1, sets[(t + 1) % 2])
                rms_apply(t, gname, l, dst_fn(t), dstB_fn(t), st=sets[t % 2])
                if t + 2 < nt:
                    rms_sq(t + 2, sets[t % 2])

        def ffn(l, f):
            phase_reset()
            fb = alloc_ffn()
            hT, hTB, actT, actB, sgR, wguR, wdR = fb.hT, fb.hTB, fb.actT, fb.actB, fb.sgR, fb.wguR, fb.wdR
            gname = "g_ffn%d" % f
            ntt = S // TF
            nsub = TF // 512
            hsel = [0]

            def norm(tt):
                i = hsel[0] % 2
                hsel[0] += 1
                for sub in range(nsub):
                    t5 = tt * nsub + sub
                    rms_stats(t5)
                    rms_apply(t5, gname, l, lambda kc: hT[i][:, kc, sub * 512:(sub + 1) * 512], lambda kc: [hTB[i]])
                return i

            cur = norm(0)
            for tt in range(ntt):
                h, hB = hT[cur], hTB[cur]
                for j in range(NJ):
                    ws, wB, key = wguR.next()
                    c.dma("pool", key, ws, wgu_d[f][l, j], writes=[wB])
                    for sub in range(nsub):
                        ss_ = slice(sub * 512, (sub + 1) * 512)
                        pg, pgB = psGU.next()
                        pu, puB = psGU.next()
                        for g, (ps, pB) in enumerate(((pg, pgB), (pu, puB))):
                            for kc in range(8):
                                o = (g * 8 + kc) * 128
                                c.mm(ps[:, :], ws[:, o:o + 128], h[:, kc, ss_], start=(kc == 0), stop=(kc == 7),
                                     reads=[wB, hB], writes=[pB])
                        s_, sB_ = sgR.next()
                        c.act(s_, pg[:, :], AF.Silu, reads=[pgB], writes=[sB_])
                        c.tt("dve", actT[:, j, ss_], pu[:, :], s_, ALU.mult, reads=[puB, sB_], writes=[actB[j]])
                    if tt + 1 < ntt and nsub == 2:
                        t5a, t5b = (tt + 1) * nsub, (tt + 1) * nsub + 1
                        if j == 3:
                            nxt = hsel[0] % 2
                            hsel[0] += 1
                            rms_sq(t5a)
                        elif j == 6:
                            rms_mm(t5a)
                        elif j == 8:
                            rms_apply(t5a, gname, l, lambda kc: hT[nxt][:, kc, 0:512], lambda kc: [hTB[nxt]])
                        elif j == 10:
                            rms_sq(t5b)
                        elif j == 13:
                            rms_mm(t5b)
                        elif j == 15:
                            rms_apply(t5b, gname, l, lambda kc: hT[nxt][:, kc, 512:1024], lambda kc: [hTB[nxt]])
                if tt + 1 < ntt and nsub != 2:
                    nxt = norm(tt + 1)
                for m in range(8):
                    ws, wB, key = wdR.next()
                    c.dma("pool", key, ws, wd_d[f][l, m], writes=[wB])
                    for sub in range(nsub):
                        ss_ = slice(sub * 512, (sub + 1) * 512)
                        t5 = tt * nsub + sub
                        tok = slice(t5 * 512, (t5 + 1) * 512)
                        ps, pB = psDN.next()
                        for j in range(NJ):
                            c.mm(ps[:, :], ws[:, j * 128:(j + 1) * 128], actT[:, j, ss_], start=(j == 0), stop=(j == NJ - 1),
                                 reads=[wB, actB[j]], writes=[pB])
                        c.stt(xT[:, m, tok], ps[:, :], 0.5, xT[:, m, tok], ALU.mult, ALU.add,
                              reads=[pB], writes=[xB[m][t5]])
                if tt + 1 < ntt:
                    cur = nxt

        def final_norm(l):
            phase_reset()
            alloc_norm()
            norm_pipe(S // 512, "g_out", l,
                      lambda t: (lambda kc: xT[:, kc, t * 512:(t + 1) * 512]),
                      lambda t: (lambda kc: [xB[kc][t]]))

        psM = Rot([(psum[i], psB[i]) for i in range(8)])
        CMUL = float(128 ** -0.5)

        def wload(R_, src):
            ws, wB, key = R_.next()
            n = src.shape[-1]
            c.dma("pool", key, ws[:, 0:n], src, writes=[wB])
            return ws, wB

        def mixer(l, seq, mix_attn=True, mix_mlstm=True):
            phase_reset()
            hTm = carve([8, S], BF16)
            hTmB = [Buf(f"hTm{t}") for t in range(4)]
            y_mT = carve([4, S], BF16)
            ymB = [Buf(f"ym{t}") for t in range(4)]
            wmR = Rot([(carve([1024], BF16), Buf(f"wm{i}"), f"wm{i}") for i in range(4)])
            mark0 = apos[0]
            alloc_norm()
            norm_pipe(4, "g_mix", l,
                      lambda t: (lambda kc: hTm[:, kc, t * 512:(t + 1) * 512]),
                      lambda t: (lambda kc: [hTmB[t]]))

            phase_reset(mark0)
            if mix_mlstm:
                mlstm(l, hTm, hTmB, y_mT, ymB, wmR)
            phase_reset(mark0)
            y_aT = carve([4, S], BF16)
            yaB = [Buf(f"ya{t}") for t in range(4)]
            if mix_attn:
                attention(l, seq, hTm, hTmB, y_aT, yaB, wmR)
            phase_reset(apos[0] if not mix_attn else mark_after_ya[0])
            merge(l, hTm, hTmB, y_aT, yaB, y_mT, ymB, wmR, mix_attn, mix_mlstm)

        mark_after_ya = [0]

        def mlstm(l, hTm, hTmB, y_mT, ymB, wmR):
            gB = Buf("gates")
            dcy = carve([2, 16, 4], F32)
            Eg = carve([2, 16, 4], F32)
            flo = carve([2, 16, 4], F32)
            HS = []
            for i in range(2):
                st = {}
                st["qT"] = carve([S], BF16)
                st["kT"] = carve([S], BF16)
                st["qB"] = Buf(f"qT{i}")
                st["kB"] = Buf(f"kT{i}")
                st["k_tm"] = carve([16, 128], BF16)
                st["ktB"] = Buf(f"k_tm{i}")
                st["vaug"] = carve([16, 130], BF16)
                st["vB"] = Buf(f"vaug{i}")
                HS.append(st)
            ub = carve([S], BF16)
            ubB = Buf("ub")
            cacc = carve([S], F32)
            caB = Buf("cacc")
            oflat = carve([2 * 16 * 130], F32)
            outb = oflat.rearrange("p (a b c) -> p a b c", a=2, b=16)
            gpos = [0]

            def gsub(shape):
                n = int(np.prod(shape))
                v = oflat[:, gpos[0]:gpos[0] + n]
                gpos[0] += n
                assert gpos[0] <= 2 * 16 * 130
                if len(shape) == 2:
                    v = v.rearrange("p (a b) -> p a b", a=shape[0])
                elif len(shape) == 3:
                    v = v.rearrange("p (a b c) -> p a b c", a=shape[0], b=shape[1])
                return v

            ga = gsub([16, 16])
            t1 = gsub([16, 16])
            t2 = gsub([16, 16])
            lfa = gsub([16, 16])
            u = gsub([2, 16, 4])
            bsb = gsub([2, 16, 4])
            btot = gsub([2, 16, 4])
            umx = gsub([2, 16, 4])
            mref = gsub([2, 16, 4])
            min_ = gsub([2, 17, 4])
            diag = gsub([128])
            umc = gsub([2])
            oB = [Buf("outb0"), Buf("outb1")]
            hacc = carve([16, 128], F32)
            haB = Buf("hacc")
            hnb = carve([16, 128], BF16)
            hnB = Buf("hnb")
            NAT = 6
            ATR = Rot([(carve([128], BF16), Buf(f"AT{i}")) for i in range(NAT)])
            vER = Rot([(carve([130], BF16), Buf(f"vE{i}")) for i in range(NAT)])
            CdR = Rot([(carve([130], BF16), Buf(f"Cd{i}")) for i in range(4)])
            Caug = [carve([130], F32) for d in range(2)]
            CaB = [Buf("Caug0"), Buf("Caug1")]
            sml = carve([8, 16], F32)
            smB = Buf("sml")
            sgo = carve([512], F32)
            sgoB = Buf("sgo")
            for st in HS:
                c.op("dve", lambda e, st=st: e.memset(st["vaug"][:, :, 128:130], 1.0), writes=[st["vB"]])

            def gates_gen():
                psGt = Rot([(psum[i], psB[i]) for i in range(6)])
                wg_s, wgB = wload(wmR, wing_d[l])
                pg, pgB = psGt.next()
                for ch in range(16):
                    for kc in range(8):
                        c.mm(pg[:, ch * 16:(ch + 1) * 16], hTm[:, kc, ch * 128:(ch + 1) * 128], wg_s[:, kc * 16:(kc + 1) * 16],
                             start=(kc == 0), stop=(kc == 7), reads=[hTmB[ch // 4], wgB], writes=[pgB], inc=(kc == 7 and ch == 15))
                gb = PARs("gbias", l * 16, 16).unsqueeze(1).broadcast_to([128, 16, 16])
                pg3 = pg[:, 0:256].rearrange("p (a b) -> p a b", a=16)
                c.tt("dve", ga, pg3, gb, ALU.add, reads=[pgB, cstB], writes=[gB])
                c.stt(t1, ga, -1.0, ga, ALU.mult, ALU.max, reads=[gB], writes=[gB])
                c.act(t1, t1, AF.Exp, scale=-1.0, reads=[gB], writes=[gB])
                c.act(t1, t1, AF.Ln, bias=CFs("one"), reads=[gB, cstB], writes=[gB])
                c.ts("dve", t2, ga, 0.0, None, ALU.min, reads=[gB], writes=[gB])
                c.tt("dve", lfa, t2, t1, ALU.subtract, reads=[gB], writes=[gB])
                yield
                lfa2 = lfa.rearrange("p a b -> p (a b)")
                pcs = []
                for nm in ("LT", "UT", "ones"):
                    p_, pB_ = psGt.next()
                    c.mm(p_[:, 0:256], CFs(nm), lfa2, start=True, stop=True, reads=[gB, cstB], writes=[pB_])
                    pcs.append((p_[:, 0:256].rearrange("p (a b) -> p a b", a=16), pB_))
                for d in range(2):
                    fo = 4 + 8 * d
                    io = 8 * d
                    c.copy("dve", bsb[:, d], pcs[d][0][:, :, fo:fo + 4], reads=[pcs[d][1]], writes=[gB])
                    c.copy("dve", btot[:, d], pcs[2][0][:, :, fo:fo + 4], reads=[pcs[2][1]], writes=[gB])
                    c.tt("dve", u[:, d], ga[:, :, io:io + 4], bsb[:, d], ALU.subtract, reads=[gB], writes=[gB])
                yield
                u2 = u.rearrange("p a b c -> p (a b c)")
                pt_, ptB = psGt.next()
                c.tr(pt_[:, 0:128], u2, CFs("ident"), reads=[gB, cstB], writes=[ptB])
                c.red(umc[:, 0:1], pt_[:, 0:128], ALU.max, reads=[ptB], writes=[gB])
                c.ts("dve", diag, CFs("ident"), umc[:, 0:1], None, ALU.mult, reads=[gB, cstB], writes=[gB])
                pr_, prB = psGt.next()
                c.mm(pr_[:, 0:128], CFs("ones"), diag, start=True, stop=True, reads=[gB, cstB], writes=[prB])
                c.copy("dve", umx.rearrange("p a b c -> p (a b c)"), pr_[:, 0:128], reads=[prB], writes=[gB])
                yield
                c.op("dve", lambda e: e.memset(min_[:, 0, 0, :], 0.0), writes=[gB])
                c.op("dve", lambda e: e.memset(min_[:, 1, 16, :], 0.0), writes=[gB])
                for ch in range(16):
                    c.tt("dve", mref[:, 0, ch, :], min_[:, 0, ch, :], umx[:, 0, ch, :], ALU.max, reads=[gB], writes=[gB])
                    c.tt("dve", min_[:, 0, ch + 1, :], btot[:, 0, ch, :], mref[:, 0, ch, :], ALU.add, reads=[gB], writes=[gB])
                    if ch % 4 == 0:
                        yield
                for ch in range(15, -1, -1):
                    c.tt("dve", mref[:, 1, ch, :], min_[:, 1, ch + 1, :], umx[:, 1, ch, :], ALU.max, reads=[gB], writes=[gB])
                    c.tt("dve", min_[:, 1, ch, :], btot[:, 1, ch, :], mref[:, 1, ch, :], ALU.add, reads=[gB], writes=[gB])
                    if ch % 4 == 0:
                        yield
                c.tt("dve", dcy[:, 0], min_[:, 0, 0:16, :], mref[:, 0], ALU.subtract, reads=[gB], writes=[gB])
                c.tt("dve", dcy[:, 1], min_[:, 1, 1:17, :], mref[:, 1], ALU.subtract, reads=[gB], writes=[gB])
                c.act(dcy, dcy, AF.Exp, reads=[gB], writes=[gB])
                c.tt("dve", Eg, u, mref, ALU.subtract, reads=[gB], writes=[gB])
                c.act(Eg, Eg, AF.Exp, reads=[gB], writes=[gB])
                c.tt("dve", flo, bsb, mref, ALU.add, reads=[gB], writes=[gB])
                c.act(flo, flo, AF.Exp, scale=-1.0, reads=[gB], writes=[gB])


                yield

            psU = Rot([(psum[6], psB[6]), (psum[7], psB[7])])

            def prep(h):
                st = HS[h % 2]
                qT, kT, k_tm, vaug = st["qT"], st["kT"], st["k_tm"], st["vaug"]
                for t_, dstT, dB in ((0, qT, st["qB"]), (1, kT, st["kB"])):
                    ws, wB = wload(wmR, winb_d[l, 6 + 4 * h + t_])
                    for tt in range(4):
                        tok = slice(tt * 512, (tt + 1) * 512)
                        p_, pB_ = psU.next()
                        for kc in range(8):
                            c.mm(p_[:, :], ws[:, kc * 128:(kc + 1) * 128], hTm[:, kc, tok], start=(kc == 0), stop=(kc == 7),
                                 reads=[wB, hTmB[tt]], writes=[pB_])
                        c.copy("act", ub[:, tok], p_[:, :], reads=[pB_], writes=[ubB])
                        yield
                    blk = t_ * 4 + h
                    w0 = PARs("convw", (l * 3 + 0) * 8 + blk)
                    w1 = PARs("convw", (l * 3 + 1) * 8 + blk)
                    w2 = PARs("convw", (l * 3 + 2) * 8 + blk)
                    b_ = PARs("convb", l * 8 + blk)
                    c.act(cacc, ub, AF.Identity, scale=w1, bias=b_, reads=[ubB, cstB], writes=[caB])
                    yield
                    c.stt(cacc[:, 1:S], ub[:, 0:S - 1], w0, cacc[:, 1:S], ALU.mult, ALU.add, reads=[ubB, cstB], writes=[caB])
                    yield
                    c.stt(cacc[:, 0:S - 1], ub[:, 1:S], w2, cacc[:, 0:S - 1], ALU.mult, ALU.add, reads=[ubB, cstB], writes=[caB])
                    c.act(dstT, cacc, AF.Silu, reads=[caB], writes=[dB])
                    yield
                ws, wB = wload(wmR, winb_d[l, 6 + 4 * h + 2])
                for c4 in range(4):
                    p_, pB_ = psU.next()
                    for k in range(4):
                        ch = c4 * 4 + k
                        for kc in range(8):
                            c.mm(p_[:, k * 128:(k + 1) * 128], hTm[:, kc, ch * 128:(ch + 1) * 128], ws[:, kc * 128:(kc + 1) * 128],
                                 start=(kc == 0), stop=(kc == 7), reads=[wB, hTmB[c4]], writes=[pB_], inc=(kc == 7 and k == 3))
                    c.copy("act", vaug[:, c4 * 4:(c4 + 1) * 4, 0:128], p_[:, :].rearrange("p (a b) -> p a b", a=4),
                           reads=[pB_], writes=[st["vB"]])
                    yield
                for c4 in range(4):
                    p_, pB_ = psU.next()
                    pbf = p_[:, :].bitcast(BF16)
                    for k in range(4):
                        ch = c4 * 4 + k
                        c.tr(pbf[:, k * 128:(k + 1) * 128], kT[:, ch * 128:(ch + 1) * 128], CBs("ident"),
                             reads=[st["kB"], cstB], writes=[pB_], inc=(k == 3))
                    c.copy("dve", k_tm[:, c4 * 4:(c4 + 1) * 4, :], pbf[:, 0:512].rearrange("p (a b) -> p a b", a=4),
                           reads=[pB_], writes=[st["ktB"]])
                    yield

            def bank_pair_slots(banks, width):
                items = []
                for b_ in banks:
                    bb = Buf(f"psbank{b_}")
                    for q in range(2):
                        items.append((psum[b_][:, q * 256:q * 256 + width], bb))
                return Rot(items)

            psSl = Rot([(psum[b_][:, 0:128], psB[b_]) for b_ in (0, 1)])
            psGl = Rot([(psum[b_][:, 0:129], psB[b_]) for b_ in (2, 3)])
            psOl = Rot([(psum[b_][:, 0:129], psB[b_]) for b_ in (4, 5)])

            def stageA(h, d, i):
                st = HS[h % 2]
                ch = i if d == 0 else 15 - i
                cs = slice(ch * 128, (ch + 1) * 128)
                Ecol = Eg[:, d, ch, h:h + 1]
                pS, pSB = psSl.next()
                c.mm(pS, st["kT"][:, cs], st["qT"][:, cs], start=True, stop=True, reads=[st["qB"], st["kB"]], writes=[pSB])
                at, atB = ATR.next()
                c.stt(at, pS, Ecol, CBs("maskF" if d == 0 else "maskB"), ALU.mult, ALU.mult,
                      reads=[pSB, gB, cstB], writes=[atB])
                ve, veB = vER.next()
                c.ts("pool", ve, st["vaug"][:, ch, :], Ecol, 1.0, ALU.mult, ALU.mult, reads=[st["vB"], gB], writes=[veB])
                return (at, atB, ve, veB)

            def emit_cd(h, d, i):
                ch = i if d == 0 else 15 - i
                dcol = dcy[:, d, ch, h:h + 1]
                cd, cdB = CdR.next()
                c.act(cd[:, 0:129], Caug[d][:, 0:129], AF.Copy, scale=dcol, reads=[CaB[d], gB], writes=[cdB])
                return (cd, cdB)

            def emit_G(h, d, i, a_):
                st = HS[h % 2]
                at, atB, ve, veB = a_
                ch = i if d == 0 else 15 - i
                pG, pGB = psGl.next()
                c.mm(pG, st["k_tm"][:, ch, :], ve[:, 0:129], start=True, stop=True, reads=[st["ktB"], veB], writes=[pGB])
                return (pG, pGB)

            def emit_out(h, d, i, a_, cd_):
                st = HS[h % 2]
                at, atB, ve, veB = a_
                ch = i if d == 0 else 15 - i
                cs = slice(ch * 128, (ch + 1) * 128)
                vaug = st["vaug"]
                pO, pOB = psOl.next()
                if cd_ is not None:
                    cd, cdB = cd_
                    c.mm(pO, at, vaug[:, ch, 0:129], start=True, stop=False, reads=[atB, st["vB"]], writes=[pOB])
                    c.mm(pO, st["qT"][:, cs], cd[:, 0:129], start=False, stop=True, reads=[st["qB"], cdB], writes=[pOB])
                else:
                    c.mm(pO, at, vaug[:, ch, 0:129], start=True, stop=True, reads=[atB, st["vB"]], writes=[pOB])
                return (pO, pOB)

            def emit_stt(h, d, i, g_):
                ch = i if d == 0 else 15 - i
                dcol = dcy[:, d, ch, h:h + 1]
                pG, pGB = g_
                if i == 0:
                    c.copy("dve", Caug[d][:, 0:129], pG, reads=[pGB], writes=[CaB[d]])
                elif i < 15:
                    c.stt(Caug[d][:, 0:129], Caug[d][:, 0:129], dcol, pG, ALU.mult, ALU.add,
                          reads=[pGB, gB], writes=[CaB[d]])

            ocp = [0]

            def emit_ocopy(d, i, o_):
                ch = i if d == 0 else 15 - i
                pO, pOB = o_
                ocp[0] += 1
                c.copy("act" if ocp[0] % 2 else "dve", outb[:, d, ch, 0:129], pO, reads=[pOB], writes=[oB[d]])

            def epi1(h):
                for d in range(2):
                    dn = sml[:, d, :]
                    c.ts("dve", dn, outb[:, d, :, 128], CMUL, None, ALU.mult, reads=[oB[d]], writes=[smB])
                    c.stt(dn, dn, -1.0, dn, ALU.mult, ALU.max, reads=[smB], writes=[smB])
                    c.tt("dve", dn, dn, flo[:, d, :, h], ALU.max, reads=[gB], writes=[smB])
                    c.recip(dn, dn, reads=[smB], writes=[smB])
                    c.ts("dve", dn, dn, CMUL, None, ALU.mult, reads=[smB], writes=[smB])
                c.tt("dve", hacc, outb[:, 0, :, 0:128], sml[:, 0, :].unsqueeze(2).broadcast_to([128, 16, 128]), ALU.mult,
                     reads=[smB, oB[0]], writes=[haB])
                c.tt("dve", outb[:, 1, :, 0:128], outb[:, 1, :, 0:128], sml[:, 1, :].unsqueeze(2).broadcast_to([128, 16, 128]),
                     ALU.mult, reads=[smB], writes=[oB[1]])
                c.tt("dve", hacc, hacc, outb[:, 1, :, 0:128], ALU.add, reads=[oB[1]], writes=[haB])

            def epi2(h):
                s1 = sml[:, 2, :]
                s2 = sml[:, 3, :]
                mu = sml[:, 4, :]
                var = sml[:, 5, :]
                nmr = sml[:, 6, :]
                c.red(s1, hacc, ALU.add, reads=[haB], writes=[smB])
                yield
                for c4 in range(4):
                    for k in range(4):
                        ch = c4 * 4 + k
                        c.act(hnb[:, ch, :], hacc[:, ch, :], AF.Square, accum_out=s2[:, ch:ch + 1], reads=[haB], writes=[hnB, smB])
                    yield
                c.ts("dve", mu, s1, 1.0 / 128, None, ALU.mult, reads=[smB], writes=[smB])
                c.tt("dve", var, mu, mu, ALU.mult, reads=[smB], writes=[smB])
                c.stt(var, s2, 1.0 / 128, var, ALU.mult, ALU.subtract, reads=[smB], writes=[smB])
                c.act(var, var, AF.Ln, bias=CFs("eps"), reads=[smB, cstB], writes=[smB])
                c.act(var, var, AF.Exp, scale=-0.5, reads=[smB], writes=[smB])
                yield
                c.stt(nmr, mu, -1.0, var, ALU.mult, ALU.mult, reads=[smB], writes=[smB])
                yield
                for c4 in range(4):
                    for k in range(4):
                        ch = c4 * 4 + k
                        c.act(hnb[:, ch, :], hacc[:, ch, :], AF.Identity, scale=var[:, ch:ch + 1], bias=nmr[:, ch:ch + 1],
                              reads=[haB, smB], writes=[hnB])
                    yield
                ws, wB = wload(wmR, winb_d[l, 6 + 4 * h + 3])
                for tt in range(4):
                    tok = slice(tt * 512, (tt + 1) * 512)
                    p_, pB_ = psU.next()
                    for kc in range(8):
                        c.mm(p_[:, :], ws[:, kc * 128:(kc + 1) * 128], hTm[:, kc, tok], start=(kc == 0), stop=(kc == 7),
                             reads=[wB, hTmB[tt]], writes=[pB_])
                    c.act(sgo, p_[:, :], AF.Tanh, scale=0.5, reads=[pB_], writes=[sgoB])
                    c.ts("dve", sgo, sgo, 0.5, 0.5, ALU.mult, ALU.add, reads=[sgoB], writes=[sgoB])
                    p2, p2B = psU.next()
                    pbf = p2[:, :].bitcast(BF16)
                    for k in range(4):
                        ch = tt * 4 + k
                        c.tr(pbf[:, k * 128:(k + 1) * 128], hnb[:, ch, :], CBs("ident"), reads=[hnB, cstB], writes=[p2B], inc=(k == 3))
                    c.stt(y_mT[:, h, tok], pbf[:, 0:512], PARs("gmn", l * 4 + h), sgo, ALU.mult, ALU.mult,
                          reads=[p2B, sgoB, cstB], writes=[ymB[tt]])
                    yield

            def exhaust(g):
                for _ in g:
                    pass

            def pump(gens, n=1):
                for g in gens:
                    for _ in range(n):
                        try:
                            next(g)
                        except StopIteration:
                            break

            LA = 2
            g1, g2 = gates_gen(), prep(0)
            alive = [g1, g2]
            while alive:
                for g in list(alive):
                    try:
                        next(g)
                    except StopIteration:
                        alive.remove(g)
            for d in range(2):
                oB[d].readers.extend(gB.writers + gB.readers)
            pend_epi = None
            for h in range(4):
                gens = []
                if pend_epi is not None:
                    gens.append(pend_epi)
                if h + 1 < 4:
                    gens.append(prep(h + 1))
                As = {}
                for i in range(LA):
                    for d in range(2):
                        As[(d, i)] = stageA(h, d, i)
                cds = {(0, 0): None, (1, 0): None}
                outs = {}
                for k in range(16):
                    if k + LA < 16:
                        for d in range(2):
                            As[(d, k + LA)] = stageA(h, d, k + LA)
                    if k > 0:
                        for d in range(2):
                            emit_ocopy(d, k - 1, outs.pop((d, k - 1)))
                    pump(gens, 1)
                    gs = {}
                    for d in range(2):
                        gs[d] = emit_G(h, d, k, As[(d, k)])
                    for d in range(2):
                        outs[(d, k)] = emit_out(h, d, k, As[(d, k)], cds.pop((d, k)))
                    for d in range(2):
                        emit_stt(h, d, k, gs[d])
                        As.pop((d, k))
                    if k + 1 < 16:
                        for d in range(2):
                            cds[(d, k + 1)] = emit_cd(h, d, k + 1)
                for d in range(2):
                    emit_ocopy(d, 15, outs.pop((d, 15)))
                for g in gens:
                    exhaust(g)
                epi1(h)
                pend_epi = epi2(h)
            exhaust(pend_epi)

        def attention(l, seq, hTm, hTmB, y_aT, yaB, wmR):
            Ct = carve([S], BF16)
            St = carve([S], BF16)
            csB = Buf("cs")
            mark_after_ya[0] = apos[0]
            mark2 = apos[0]
            posi = carve([S], I32)
            ang = carve([S], F32)
            tB = Buf("tbl")
            c.dma("sp", "pos", posi, pos_d[seq], writes=[tB])
            c.copy("dve", ang, posi, reads=[tB], writes=[tB])
            c.ts("dve", ang, ang, CFs("invf"), float(1.0 / (2 * np.pi)), ALU.mult, ALU.mult, reads=[tB, cstB], writes=[tB])
            posf = posi.bitcast(F32)
            tmp = carve([S], F32)
            for (shift, dst) in ((0.0, St), (0.25, Ct)):
                c.ts("dve", posi, ang, float(shift), None, ALU.add, reads=[tB], writes=[tB])
                c.copy("dve", tmp, posi, reads=[tB], writes=[tB])
                c.stt(tmp, ang, float(shift), tmp, ALU.add, ALU.subtract, reads=[tB], writes=[tB])
                c.ts("dve", posf, tmp, 0.5, None, ALU.is_ge, reads=[tB], writes=[tB])
                c.tt("dve", tmp, tmp, posf, ALU.subtract, reads=[tB], writes=[tB])
                c.act(dst, tmp, AF.Sin, scale=float(2 * np.pi), reads=[tB], writes=[csB])
            phase_reset(mark2)
            qrT = carve([4, S], BF16)
            krT = carve([S], BF16)
            qkB_ = [[Buf(f"qr{i}_{t}") for t in range(4)] for i in range(5)]
            vA = carve([16, 2, 66], BF16)
            vAB = Buf("vA")
            zR = Rot([(carve([512], BF16), carve([512], BF16), Buf(f"z{i}")) for i in range(3)])
            tqR = Rot([(carve([512], F32), carve([512], F32), carve([512], BF16), Buf(f"tq{i}")) for i in range(3)])
            pTs = [carve([512], BF16) for i in range(9)]
            pTR = Rot([(pTs[i], Buf(f"pT{i}")) for i in range(9)])
            yat = [carve([512], BF16) for i in range(2)]
            yatB = [Buf("yat0"), Buf("yat1")]
            es_ = carve([8], F32)
            dn_ = carve([2, 4], F32)
            smB = Buf("asml")
            c.act(es_, PARs("sink", l * 8, 8), AF.Exp, reads=[cstB], writes=[smB])
            c.op("dve", lambda e: e.memset(vA[:, :, :, 64:66], 1.0), writes=[vAB])
            psP1 = Rot([(psum[i], psB[i]) for i in (0, 1, 2)])
            psP2 = Rot([(psum[i], psB[i]) for i in (3, 4)])
            psP3 = Rot([(psum[i], psB[i]) for i in (5, 6, 7)])
            units = [(bi, tt) for bi in range(5) for tt in range(4)]
            wcur = {}
            wcur[0] = wload(wmR, winb_d[l, 0])

            def st1(bi, tt):
                if tt == 0:
                    wcur[bi + 1] = wload(wmR, winb_d[l, bi + 1])
                ws, wB = wcur[bi]
                gcol = PARs("gq" if bi < 4 else "gk", l)
                tok = slice(tt * 512, (tt + 1) * 512)
                p_, pB_ = psP1.next()
                for kc in range(8):
                    c.mm(p_[:, :], ws[:, kc * 128:(kc + 1) * 128], hTm[:, kc, tok], start=(kc == 0), stop=(kc == 7),
                         reads=[wB, hTmB[tt]], writes=[pB_])
                zsq, zg, zB = zR.next()
                c.act(zsq, p_[:, :], AF.Square, reads=[pB_], writes=[zB])
                c.act(zg, p_[:, :], AF.Copy, scale=gcol, reads=[pB_, cstB], writes=[zB])
                return dict(bi=bi, tt=tt, tok=tok, zsq=zsq, zg=zg, zB=zB)

            def st2(u_):
                lnq, t1, t2, tqB = tqR.next()
                pss, pssB = psP2.next()
                c.mm(pss[:, :], CBs("blk64"), u_["zsq"], start=True, stop=True, reads=[u_["zB"], cstB], writes=[pssB])
                prz, przB = psP3.next()
                c.mm(prz[:, :], CBs("rblk"), u_["zg"], start=True, stop=True, reads=[u_["zB"], cstB], writes=[przB])
                c.act(lnq, pss[:, :], AF.Ln, bias=CFs("eps"), scale=1.0 / 64, reads=[pssB, cstB], writes=[tqB])
                c.act(lnq, lnq, AF.Exp, scale=-0.5, reads=[tqB], writes=[tqB])
                c.tt("pool", t2, u_["zg"], Ct[:, u_["tok"]], ALU.mult, reads=[u_["zB"], csB], writes=[tqB])
                u_.update(rs=lnq, t1=t1, t2=t2, tqB=tqB, prz=prz, przB=przB)

            def st3(u_):
                bi, tt, tok = u_["bi"], u_["tt"], u_["tok"]
                dst = qrT[:, bi, tok] if bi < 4 else krT[:, tok]
                t1, t2, rs, tqB = u_["t1"], u_["t2"], u_["rs"], u_["tqB"]
                c.tt("dve", t1, u_["prz"][:, :], St[:, tok], ALU.mult, reads=[u_["przB"], csB], writes=[tqB])
                c.tt("dve", t1, t1, t2, ALU.add, reads=[tqB], writes=[tqB])
                c.tt("dve", dst, t1, rs, ALU.mult, reads=[tqB], writes=[qkB_[bi][tt]])

            live = []
            for ui in range(len(units) + 2):
                if ui < len(units):
                    live.append(st1(*units[ui]))
                if 1 <= ui <= len(units):
                    st2(live[ui - 1])
                if ui >= 2:
                    st3(live[ui - 2])
            wnext = wcur[5]
            ws, wB = wnext
            for c4 in range(4):
                p_, pB_ = psM.next()
                for k in range(4):
                    ch = c4 * 4 + k
                    for kc in range(8):
                        c.mm(p_[:, k * 128:(k + 1) * 128], hTm[:, kc, ch * 128:(ch + 1) * 128], ws[:, kc * 128:(kc + 1) * 128],
                             start=(kc == 0), stop=(kc == 7), reads=[wB, hTmB[c4]], writes=[pB_], inc=(kc == 7 and k == 3))
                c.copy("act", vA[:, c4 * 4:(c4 + 1) * 4, :, 0:64],
                       p_[:, :].rearrange("p (a b d) -> p a b d", a=4, b=2), reads=[pB_], writes=[vAB])
            iters = [(n, kvh) for n in range(16) for kvh in range(2)]

            def emit_S(n, kvh):
                qs = slice(n * 128, (n + 1) * 128)
                pr = slice(kvh * 64, kvh * 64 + 64)
                js = [j for j in (n - 1, n, n + 1) if 0 <= j < 16]
                pts = []
                for j in js:
                    ks = slice(j * 128, (j + 1) * 128)
                    ps_, psB_ = psSa.next()
                    rd = [qkB_[i][n // 4] for i in range(4)] + [qkB_[4][j // 4]]
                    c.mm(ps_[:, :], krT[pr, ks], qrT[pr, :, qs], start=True, stop=(j == n), reads=rd, writes=[psB_])
                    if j != n:
                        c.mm(ps_[:, :], CBs("ident"), CBs("mprev" if j < n else "mnext"), start=False, stop=True,
                             reads=[cstB], writes=[psB_])
                    pt, ptB_ = pTR.next()
                    c.act(pt, ps_[:, :], AF.Exp, scale=0.125, reads=[psB_], writes=[ptB_])
                    pts.append((j, pt, ptB_))
                return pts

            def emit_PV(n, kvh, pts):
                ya, yB_ = yat[n % 2], yatB[n % 2]
                po, poB = psOa.next()
                po3 = po[:, :].rearrange("p (g d) -> p g d", g=4)
                for g in range(4):
                    for idx, (j, pt, ptB_) in enumerate(pts):
                        c.mm(po3[:, g, 0:65], pt[:, g * 128:(g + 1) * 128], vA[:, j, kvh, 0:65], start=(idx == 0),
                             stop=(idx == len(pts) - 1), reads=[ptB_, vAB], writes=[poB],
                             inc=(g == 3 and idx == len(pts) - 1))
                dn = dn_[:, kvh, :]
                c.tt("dve", dn, po3[:, :, 64], es_[:, kvh * 4:(kvh + 1) * 4], ALU.add, reads=[poB, smB], writes=[smB])
                c.recip(dn, dn, reads=[smB], writes=[smB])
                c.tt("dve", ya[:, kvh * 256:(kvh + 1) * 256].rearrange("p (g d) -> p g d", g=4), po3[:, :, 0:64],
                     dn.unsqueeze(2).broadcast_to([128, 4, 64]), ALU.mult, reads=[poB, smB], writes=[yB_])
                return n if kvh == 1 else None

            def emit_TR(n):
                ya, yB_ = yat[n % 2], yatB[n % 2]
                if True:
                    qs = slice(n * 128, (n + 1) * 128)
                    p2, p2B = psTa.next()
                    pbf = p2[:, :].bitcast(BF16)
                    for k in range(4):
                        c.tr(pbf[:, k * 128:(k + 1) * 128], ya[:, k * 128:(k + 1) * 128], CBs("ident"), reads=[yB_, cstB],
                             writes=[p2B], inc=(k == 3))
                    c.copy("act", y_aT[:, :, qs], pbf[:, 0:512].rearrange("p (a b) -> p a b", a=4), reads=[p2B], writes=[yaB[n // 4]])

            psSa = Rot([(psum[i], psB[i]) for i in (0, 1, 2, 3, 4)])
            psOa = Rot([(psum[i], psB[i]) for i in (5, 6)])
            psTa = Rot([(psum[7], psB[7])])
            prev = None
            pend_tr = None
            for (n, kvh) in iters:
                pts = emit_S(n, kvh)
                if pend_tr is not None:
                    emit_TR(pend_tr)
                    pend_tr = None
                if prev is not None:
                    pend_tr = emit_PV(*prev)
                prev = (n, kvh, pts)
            r_ = emit_PV(*prev)
            if pend_tr is not None:
                emit_TR(pend_tr)
            if r_ is not None:
                emit_TR(r_)

        def merge(l, hTm, hTmB, y_aT, yaB, y_mT, ymB, wmR, mix_attn, mix_mlstm):
            TM = 1024
            nsub = TM // 512
            tR = Rot([(carve([512], F32), Buf(f"mt{i}")) for i in range(4)])
            aR = Rot([(carve([512], F32), Buf(f"ma{i}")) for i in range(4)])
            mgT = carve([8, TM], BF16)
            mgB = [Buf(f"mg{m}") for m in range(8)]
            for tt in range(S // TM):
                for m in range(8):
                    parts = [[] for _ in range(nsub)]
                    for (on, yT, yB, wsrc, gblk) in ((mix_attn, y_aT, yaB, wba_d, 22 + m),
                                                     (mix_mlstm, y_mT, ymB, wbm_d, 30 + m)):
                        if not on:
                            continue
                        ws, wB = wload(wmR, wsrc[l, m])
                        ws2, wB2 = wload(wmR, winb_d[l, gblk])
                        for sub in range(nsub):
                            t5 = tt * nsub + sub
                            tok = slice(t5 * 512, (t5 + 1) * 512)
                            pb_, pbB = psM.next()
                            for kc in range(4):
                                c.mm(pb_[:, :], ws[:, kc * 128:(kc + 1) * 128], yT[:, kc, tok], start=(kc == 0), stop=(kc == 3),
                                     reads=[wB, yB[t5]], writes=[pbB])
                            pg_, pgB_ = psM.next()
                            for kc in range(8):
                                c.mm(pg_[:, :], ws2[:, kc * 128:(kc + 1) * 128], hTm[:, kc, tok], start=(kc == 0), stop=(kc == 7),
                                     reads=[wB2, hTmB[t5]], writes=[pgB_])
                            tb, tbB = tR.next()
                            ab, abB = aR.next()
                            c.act(tb, pg_[:, :], AF.Tanh, scale=0.5, reads=[pgB_], writes=[tbB])
                            c.stt(ab, tb, 1.0, pb_[:, :], ALU.add, ALU.mult, reads=[pbB, tbB], writes=[abB])
                            parts[sub].append((ab, abB))
                    for sub in range(nsub):
                        ss_ = slice(sub * 512, (sub + 1) * 512)
                        if len(parts[sub]) == 2:
                            c.tt("dve", mgT[:, m, ss_], parts[sub][0][0], parts[sub][1][0], ALU.add,
                                 reads=[parts[sub][0][1], parts[sub][1][1]], writes=[mgB[m]])
                        else:
                            c.copy("dve", mgT[:, m, ss_], parts[sub][0][0], reads=[parts[sub][0][1]], writes=[mgB[m]])
                for m2 in range(8):
                    ws, wB = wload(wmR, wo_d[l, m2])
                    for sub in range(nsub):
                        ss_ = slice(sub * 512, (sub + 1) * 512)
                        t5 = tt * nsub + sub
                        tok = slice(t5 * 512, (t5 + 1) * 512)
                        po_, poB_ = psM.next()
                        for m in range(8):
                            c.mm(po_[:, :], ws[:, m * 128:(m + 1) * 128], mgT[:, m, ss_], start=(m == 0), stop=(m == 7),
                                 reads=[wB, mgB[m]], writes=[poB_])
                        c.stt(xT[:, m2, tok], po_[:, :], 0.5, xT[:, m2, tok], ALU.mult, ALU.add, reads=[poB_], writes=[xB[m2][t5]])

        outBs = []
        for seq in range(nseq):
            load_x(seq)
            for l in layers:
                if do_ffn1:
                    ffn(l, 1)
                if do_mixer:
                    mixer(l, seq, mix_attn, mix_mlstm)
                if do_ffn2:
                    ffn(l, 2)
                if do_norm:
                    final_norm(l)
            outBs.append(store_x(seq))
        c.final_wait("sp", outBs)
        c.emit()
    return nc


def make_in_maps(inputs, nseq=SEQ_PER_CORE, ncores=NCORES):
    w = prep_weights(inputs)
    cbv, cfv = make_consts()
    x = np.ascontiguousarray(inputs["x"], dtype=np.float32)
    pos = np.ascontiguousarray(inputs["positions"], dtype=np.int32)
    in_maps = []
    for ci in range(ncores):
        m = dict(w)
        m["x"] = x[ci * nseq:(ci + 1) * nseq]
        m["pos"] = np.ascontiguousarray(np.broadcast_to(pos[ci * nseq:(ci + 1) * nseq, None, :], (nseq, 128, S)))
        m["cb"] = cbv
        m["cf"] = cfv
        in_maps.append(m)
    return in_maps


def kernel(**inputs):
    nc = build()
    in_maps = make_in_maps(inputs)
    res = run_bass_kernel_spmd(nc, in_maps, core_ids=list(range(NCORES)))
    return np.concatenate([r["out"] for r in res.results], axis=0).astype(np.float32)
```
